# Optimizing a Trainium2 kernel written in Bass

```python
import jax, jax.numpy as jnp
from jax import lax
import numpy as np

D_MODEL = 1024
BATCH = 8
SEQ = 4096
DEPTH = 4

CTX_LEN = 256
GRID_W = 64
CONV_W = 1024
CONV_K = 31
LRU_W = 1024
LRU_HEADS = 16
LRU_HD = LRU_W // LRU_HEADS
LRU_CONV_K = 4
LRU_PAD = ((LRU_CONV_K - 1) // 2, LRU_CONV_K // 2)
LRU_C = 8.0
FFN_W = 2560
FFN_K = 3
N_MOD = 6
EPS = 1e-6
IN_W = 2 * CONV_W + 2 * LRU_W + 2 * D_MODEL
SPLIT_IN = (2 * CONV_W, 2 * CONV_W + LRU_W, 2 * CONV_W + 2 * LRU_W)

kernel_name = 'hybrid_conformer_rglru_prefix_dit'


def _rmsnorm(x, g):
    xf = x.astype(jnp.float32)
    y = xf * lax.rsqrt(jnp.mean(xf * xf, axis=-1, keepdims=True) + EPS)
    return (y * g.astype(jnp.float32)).astype(x.dtype)


def _layernorm(x, g, b):
    xf = x.astype(jnp.float32)
    mu = jnp.mean(xf, axis=-1, keepdims=True)
    var = jnp.mean(jnp.square(xf - mu), axis=-1, keepdims=True)
    y = (xf - mu) * lax.rsqrt(var + EPS)
    return (y * g.astype(jnp.float32) + b.astype(jnp.float32)).astype(x.dtype)


def _dwconv1d(x, w, pad):
    return lax.conv_general_dilated(
        x, w.astype(x.dtype)[:, None, :], window_strides=(1,), padding=[pad],
        dimension_numbers=('NWC', 'WIO', 'NWC'), feature_group_count=x.shape[-1])


def _dwconv_grid(x, w, rows):
    b, t, ch = x.shape
    xg = x.reshape(b, rows, GRID_W, ch)
    y = lax.conv_general_dilated(
        xg, w.astype(x.dtype)[:, :, None, :], window_strides=(1, 1),
        padding=[(FFN_K // 2, FFN_K // 2), (FFN_K // 2, FFN_K // 2)],
        dimension_numbers=('NHWC', 'HWIO', 'NHWC'), feature_group_count=ch)
    return y.reshape(b, t, ch)


def _modulate(h, shift, scale):
    return h * (1 + scale) + shift


def _rglru_coeffs(xr, w_r, b_r, w_i, b_i, lam):
    b, t, _ = xr.shape
    xf = xr.astype(jnp.float32)
    xh = xf.reshape(b, t, LRU_HEADS, LRU_HD)
    r = jax.nn.sigmoid(jnp.einsum('bthi,hij->bthj', xh, w_r.astype(jnp.float32)).reshape(b, t, LRU_W)
                       + b_r.astype(jnp.float32))
    i = jax.nn.sigmoid(jnp.einsum('bthi,hij->bthj', xh, w_i.astype(jnp.float32)).reshape(b, t, LRU_W)
                       + b_i.astype(jnp.float32))
    log_a = -LRU_C * r * jax.nn.softplus(-lam.astype(jnp.float32))
    a = jnp.exp(log_a)
    u = jnp.sqrt(-jnp.expm1(2.0 * log_a)) * (i * xf)
    return a, u


def _linear_scan(a, u, h0):
    def combine(left, right):
        a_l, b_l = left
        a_r, b_r = right
        return a_l * a_r, a_r * b_l + b_r
    a_cum, h = lax.associative_scan(combine, (a, u), axis=1)
    if h0 is None:
        return h
    return h + a_cum * h0[:, None, :]


def _scan_direction(a_c, u_c, a_l, u_l, reverse):
    if reverse:
        a_c, u_c, a_l, u_l = a_c[:, ::-1], u_c[:, ::-1], a_l[:, ::-1], u_l[:, ::-1]
    h_c = _linear_scan(a_c, u_c, None)
    h_l = _linear_scan(a_l, u_l, h_c[:, -1])
    if reverse:
        h_c, h_l = h_c[:, ::-1], h_l[:, ::-1]
    return h_c, h_l


def _conformer_conv(glu, dw, ln_g, ln_b, w_proj):
    v, g = jnp.split(glu, 2, axis=-1)
    u = _dwconv1d(v * jax.nn.sigmoid(g), dw, (CONV_K // 2, CONV_K // 2))
    return jax.nn.silu(_layernorm(u, ln_g, ln_b)) @ w_proj


def _token_mixer(hl, hc, w_in, dw_conv, ln_conv_g, ln_conv_b, w_proj_conv, lru_conv_w, lru_conv_b,
                 w_rgate, b_rgate, w_igate, b_igate, lru_lambda, w_proj_lru, w_out, need_ctx):
    glu_l, xr_l, gt_l, mg_l = jnp.split(hl @ w_in, SPLIT_IN, axis=-1)
    if need_ctx:
        glu_c, xr_c, gt_c, mg_c = jnp.split(hc @ w_in, SPLIT_IN, axis=-1)
    else:
        xr_c = hc @ w_in[:, SPLIT_IN[0]:SPLIT_IN[1]]
    xr_l = _dwconv1d(xr_l, lru_conv_w, LRU_PAD) + lru_conv_b
    xr_c = _dwconv1d(xr_c, lru_conv_w, LRU_PAD) + lru_conv_b
    rec_l = None
    rec_c = None
    for d in range(2):
        a_c, u_c = _rglru_coeffs(xr_c, w_rgate[d], b_rgate[d], w_igate[d], b_igate[d], lru_lambda[d])
        a_l, u_l = _rglru_coeffs(xr_l, w_rgate[d], b_rgate[d], w_igate[d], b_igate[d], lru_lambda[d])
        h_c, h_l = _scan_direction(a_c, u_c, a_l, u_l, reverse=(d == 1))
        rec_l = h_l if rec_l is None else rec_l + h_l
        rec_c = h_c if rec_c is None else rec_c + h_c

    def merge(glu, rec, gt, mg):
        y_a = _conformer_conv(glu, dw_conv, ln_conv_g, ln_conv_b, w_proj_conv)
        y_b = (rec.astype(gt.dtype) * jax.nn.gelu(gt)) @ w_proj_lru
        g_a, g_b = jnp.split(mg, 2, axis=-1)
        return (jax.nn.sigmoid(g_a) * y_a + jax.nn.sigmoid(g_b) * y_b) @ w_out

    y_l = merge(glu_l, rec_l, gt_l, mg_l)
    y_c = merge(glu_c, rec_c, gt_c, mg_c) if need_ctx else None
    return y_l, y_c


def _conv_ffn(h, w_up, dw, w_down, rows):
    z = h @ w_up
    if rows is None:
        z = _dwconv1d(z, dw[FFN_K // 2], (FFN_K // 2, FFN_K // 2))
    else:
        z = _dwconv_grid(z, dw, rows)
    v, g = jnp.split(z, 2, axis=-1)
    return (jax.nn.silu(g) * v) @ w_down


def setup_inputs(seed: int = 0) -> dict:
    key = jax.random.key(seed)
    ks = jax.random.split(key, 32)

    def nrm(k, shape, scale):
        return jax.random.normal(k, shape, jnp.float32) * scale

    def gain(k, shape):
        return 1.0 + 0.05 * jax.random.normal(k, shape, jnp.float32)

    u = jax.random.uniform(ks[21], (DEPTH, 2, LRU_W), jnp.float32, minval=0.9, maxval=0.999)
    a = u ** (1.0 / LRU_C)
    lam = jnp.log(a) - jnp.log1p(-a)
    return {
        'x': nrm(ks[0], (BATCH, SEQ, D_MODEL), 1.0),
        'c': nrm(ks[1], (BATCH, D_MODEL), 1.0),
        'ctx': nrm(ks[2], (BATCH, CTX_LEN, D_MODEL), 1.0),
        'c_ctx': nrm(ks[3], (D_MODEL,), 1.0),
        'w_ada': nrm(ks[4], (DEPTH, D_MODEL, N_MOD * D_MODEL), 0.5 * D_MODEL ** -0.5),
        'b_ada': nrm(ks[5], (DEPTH, N_MOD * D_MODEL), 0.02),
        'g_pre_mix': gain(ks[6], (DEPTH, D_MODEL)),
        'g_post_mix': gain(ks[7], (DEPTH, D_MODEL)),
        'g_pre_ffn': gain(ks[8], (DEPTH, D_MODEL)),
        'g_post_ffn': gain(ks[9], (DEPTH, D_MODEL)),
        'w_in': nrm(ks[10], (DEPTH, D_MODEL, IN_W), D_MODEL ** -0.5),
        'dw_conv': nrm(ks[11], (DEPTH, CONV_K, CONV_W), CONV_K ** -0.5),
        'ln_conv_g': gain(ks[12], (DEPTH, CONV_W)),
        'ln_conv_b': nrm(ks[13], (DEPTH, CONV_W), 0.02),
        'w_proj_conv': nrm(ks[14], (DEPTH, CONV_W, D_MODEL), CONV_W ** -0.5),
        'lru_conv_w': nrm(ks[15], (DEPTH, LRU_CONV_K, LRU_W), LRU_CONV_K ** -0.5),
        'lru_conv_b': nrm(ks[16], (DEPTH, LRU_W), 0.02),
        'w_rgate': nrm(ks[17], (DEPTH, 2, LRU_HEADS, LRU_HD, LRU_HD), LRU_HD ** -0.5),
        'b_rgate': nrm(ks[18], (DEPTH, 2, LRU_W), 0.02),
        'w_igate': nrm(ks[19], (DEPTH, 2, LRU_HEADS, LRU_HD, LRU_HD), LRU_HD ** -0.5),
        'b_igate': nrm(ks[20], (DEPTH, 2, LRU_W), 0.02),
        'lru_lambda': lam,
        'w_proj_lru': nrm(ks[22], (DEPTH, LRU_W, D_MODEL), LRU_W ** -0.5),
        'w_out': nrm(ks[23], (DEPTH, D_MODEL, D_MODEL), D_MODEL ** -0.5),
        'w_up': nrm(ks[24], (DEPTH, D_MODEL, 2 * FFN_W), D_MODEL ** -0.5),
        'dw_ffn': nrm(ks[25], (DEPTH, FFN_K, FFN_K, 2 * FFN_W), 1.0 / FFN_K),
        'w_down': nrm(ks[26], (DEPTH, FFN_W, D_MODEL), FFN_W ** -0.5),
    }


def reference(x, c, ctx, c_ctx, w_ada, b_ada, g_pre_mix, g_post_mix, g_pre_ffn, g_post_ffn,
              w_in, dw_conv, ln_conv_g, ln_conv_b, w_proj_conv, lru_conv_w, lru_conv_b,
              w_rgate, b_rgate, w_igate, b_igate, lru_lambda, w_proj_lru, w_out,
              w_up, dw_ffn, w_down):
    rows = x.shape[1] // GRID_W
    s_lat = jax.nn.silu(c)
    s_ctx = jax.nn.silu(c_ctx)
    for l in range(DEPTH):
        need_ctx = l < DEPTH - 1
        sh1, sc1, gt1, sh2, sc2, gt2 = jnp.split((s_lat @ w_ada[l] + b_ada[l])[:, None, :], N_MOD, axis=-1)
        csh1, csc1, cgt1, csh2, csc2, cgt2 = jnp.split((s_ctx @ w_ada[l] + b_ada[l])[None, None, :], N_MOD, axis=-1)

        hl = _modulate(_rmsnorm(x, g_pre_mix[l]), sh1, sc1)
        hc = _modulate(_rmsnorm(ctx, g_pre_mix[l]), csh1, csc1)
        y_l, y_c = _token_mixer(hl, hc, w_in[l], dw_conv[l], ln_conv_g[l], ln_conv_b[l], w_proj_conv[l],
                                lru_conv_w[l], lru_conv_b[l], w_rgate[l], b_rgate[l], w_igate[l], b_igate[l],
                                lru_lambda[l], w_proj_lru[l], w_out[l], need_ctx)
        x = x + gt1 * _rmsnorm(y_l, g_post_mix[l])

        h = _modulate(_rmsnorm(x, g_pre_ffn[l]), sh2, sc2)
        x = x + gt2 * _rmsnorm(_conv_ffn(h, w_up[l], dw_ffn[l], w_down[l], rows), g_post_ffn[l])

        if need_ctx:
            ctx = ctx + cgt1 * _rmsnorm(y_c, g_post_mix[l])
            hc2 = _modulate(_rmsnorm(ctx, g_pre_ffn[l]), csh2, csc2)
            ctx = ctx + cgt2 * _rmsnorm(_conv_ffn(hc2, w_up[l], dw_ffn[l], w_down[l], None), g_post_ffn[l])
    return x
```

```python
import numpy as np
import concourse.bass as bass
import concourse.mybir as mybir
from concourse.bass_utils import run_bass_kernel_spmd

F32 = mybir.dt.float32
BF16 = mybir.dt.bfloat16
AF = mybir.ActivationFunctionType
ALU = mybir.AluOpType

L = 4
D = 1024
T = 4096
TC = 256
KC = 8
NT = 512
GW = 64
FFN = 2560
FC = 20
EPS = 1e-6
CONV_K = 31
INW = 6144

SEM_ROT = 24000
DMA_RING = 12

_off = {}
_o = 0
for _nm, _w in [("g_pre_mix", 8), ("g_post_mix", 8), ("g_pre_ffn", 8), ("g_post_ffn", 8),
                ("ln_g", 8), ("ln_b", 8), ("lru_b", 8), ("br0", 8), ("br1", 8), ("bi0", 8), ("bi1", 8),
                ("lam0", 8), ("lam1", 8), ("b_ada", 48), ("lru_w", 32), ("dw_conv", 248), ("dw_ffn", 360)]:
    _off[_nm] = _o
    _o += _w
NV = _o


class _Eng:
    def __init__(self, fw, name, handle, is_pe=False):
        self.fw = fw
        self.name = name
        self.h = handle
        self.is_pe = is_pe
        self.ops = []
        self.sem = None
        self.count = 0
        self.known = {}
        self.ring = []
        self.dma_i = 0

    def new_sem(self):
        self.sem = self.fw.alloc_sem(self.name)
        self.count = 0


class FW:
    def __init__(self, nc):
        self.nc = nc
        self.nsem = 0
        self.state = {}
        self.pe = _Eng(self, "pe", nc.tensor, is_pe=True)
        self.act = _Eng(self, "act", nc.scalar)
        self.dve = _Eng(self, "dve", nc.vector)
        self.pool = _Eng(self, "pool", nc.gpsimd)
        self.sp = _Eng(self, "sp", nc.sync)
        self.engs = [self.pe, self.act, self.dve, self.pool, self.sp]
        for e in self.engs:
            e.new_sem()
        self.n_ops = 0

    def alloc_sem(self, name):
        s = self.nc.alloc_semaphore(name=f"s_{name}_{self.nsem}")
        self.nsem += 1
        return s

    def _deps(self, eng, reads, writes):
        toks = []
        for k in reads:
            st = self.state.get(k)
            if st and st[0] is not None:
                toks.append(st[0])
        for k in writes:
            st = self.state.get(k)
            if st:
                if st[0] is not None:
                    toks.append(st[0])
                toks.extend(st[1])
        waits = []
        for (sem, val) in toks:
            if sem is eng.sem:
                if eng.is_pe:
                    continue
                if val < eng.count - 1:
                    continue
            kv = eng.known.get(id(sem), 0)
            if kv >= val:
                continue
            eng.known[id(sem)] = val
            waits.append((sem, val))
        return waits

    def _commit(self, tok, reads, writes):
        for k in reads:
            st = self.state.setdefault(k, [None, []])
            st[1].append(tok)
            if len(st[1]) > 64:
                del st[1][:32]
        for k in writes:
            self.state[k] = [tok, []]

    def op(self, eng, fn, reads=(), writes=(), signal=True):
        waits = self._deps(eng, reads, writes)
        if signal:
            if eng.count >= SEM_ROT:
                eng.new_sem()
            eng.count += 1
            tok = (eng.sem, eng.count)
            inc = (eng.sem, 1)
        else:
            tok = (eng.sem, eng.count + 1)
            inc = None
        eng.ops.append((waits, fn, inc))
        self._commit(tok, reads, writes)
        self.n_ops += 1
        return tok

    def dma(self, eng, out, in_, reads=(), writes=()):
        if not eng.ring:
            eng.ring = [[self.alloc_sem(eng.name + "_dma"), 0] for _ in range(DMA_RING)]
        slot = eng.ring[eng.dma_i % DMA_RING]
        eng.dma_i += 1
        if slot[1] >= SEM_ROT:
            extra = [(slot[0], slot[1])]
            slot[0] = self.alloc_sem(eng.name + "_dma")
            slot[1] = 0
        else:
            extra = [(slot[0], slot[1])] if slot[1] > 0 else []
        waits = self._deps(eng, reads, writes)
        for (sem, val) in extra:
            if eng.known.get(id(sem), 0) < val:
                eng.known[id(sem)] = val
                waits.append((sem, val))
        slot[1] += 16
        tok = (slot[0], slot[1])
        eng.ops.append((waits, lambda h, o=out, i=in_: h.dma_start(out=o, in_=i), (slot[0], 16)))
        self._commit(tok, reads, writes)
        self.n_ops += 1
        return tok

    def finish(self):
        waits = []
        for e in self.engs:
            for (sem, val) in e.ring:
                if val > 0:
                    waits.append((sem, val))
        self.sp.ops.append((waits, None, None))

    def emit(self):
        nc = self.nc
        with nc.Block() as block:
            def run(eng):
                def body(h):
                    for (waits, fn, inc) in eng.ops:
                        for (sem, val) in waits:
                            h.wait_ge(sem, val)
                        if fn is None:
                            continue
                        ins = fn(h)
                        if inc is not None:
                            ins.then_inc(inc[0], inc[1])
                return body
            block.tensor(run(self.pe))
            block.scalar(run(self.act))
            block.vector(run(self.dve))
            block.gpsimd(run(self.pool))
            block.sync(run(self.sp))


NLAYERS = L
DBG = 99
DEBUG_OUT = False
_DBG_RES = {}
STOP = None


def build_program():
    nc = bass.Bass("TRN2", target_bir_lowering=False)
    fw = FW(nc)
    pe, act, dve, pool, sp = fw.pe, fw.act, fw.dve, fw.pool, fw.sp

    def din(name, shape, dt=F32):
        return nc.dram_tensor(name, shape, dt, kind="ExternalInput").ap()

    def dscr(name, shape, dt=F32):
        return nc.dram_tensor(name, shape, dt, kind=("ExternalOutput" if DEBUG_OUT else "Internal")).ap()

    xin = din("xin", [D, T])
    ctxin = din("ctxin", [D, TC])
    cc = din("cc", [128, 16])
    vecs_d = din("vecs", [L, 128, NV])
    ident_d = din("ident", [128, 128])
    w_ada = din("w_ada", [L, D, INW])
    w_in = din("w_in", [L, D, INW])
    w_pc = din("w_proj_conv", [L, D, D])
    w_rg = din("w_rgate", [L, 2, 16, 64, 64])
    w_ig = din("w_igate", [L, 2, 16, 64, 64])
    w_pl = din("w_proj_lru", [L, D, D])
    w_o = din("w_out", [L, D, D])
    w_up = din("w_up", [L, D, 2 * FFN])
    w_dn = din("w_down", [L, FFN, D])
    out = nc.dram_tensor("out", [D, T], F32, kind="ExternalOutput").ap()

    class Stream:
        pass

    lat = Stream()
    ctx = Stream()
    lat.nm, lat.T, lat.col, lat.grid = "l", T, 0, True
    ctx.nm, ctx.T, ctx.col, ctx.grid = "c", TC, 1, False
    lat.tiles = [(i * NT, NT) for i in range(T // NT)]
    ctx.tiles = [(0, TC)]
    lat.xres = out
    ctx.xres = dscr("ctxres", [D, TC])
    for s in (lat, ctx):
        s.P = dscr("P" + s.nm, [D, s.T], BF16)
        s.XR = dscr("XR" + s.nm, [D, s.T])
        s.GG = dscr("GG" + s.nm, [D, s.T])
        s.SG = dscr("SG" + s.nm, [2 * D, s.T])
        s.REC = dscr("REC" + s.nm, [D, s.T])
        s.MA = dscr("MA" + s.nm, [D, s.T])
        s.Z = dscr("Z" + s.nm, [2 * FFN, s.T], BF16)

    SB_LIMIT = 229344 - 256
    apos = {"p": 16512, "base": 16512}
    acache = {}

    def sb(name, shape, dt):
        if name in acache:
            return acache[name]
        nb = (2 if dt == BF16 else 4)
        for d_ in shape[1:]:
            nb *= d_
        nb = (nb + 63) // 64 * 64
        off = apos["p"]
        assert off + nb <= SB_LIMIT, (name, off, nb)
        t_ = nc.alloc_sbuf_tensor_at(name, shape, dt, offset=off)
        apos["p"] = off + nb
        acache[name] = t_
        return t_

    class Ring:
        def __init__(self, name, shape, dtype, n):
            self.name = name
            self.t = [sb(f"{name}{i}", shape, dtype) for i in range(n)]
            self.i = 0

        def next(self):
            j = self.i % len(self.t)
            self.i += 1
            return self.t[j], (self.name, j)

    vecs = sb("vecs", [128, L, NV], F32)
    ident = sb("ident", [128, 128], F32)
    identb = sb("identb", [128, 128], BF16)
    ones = sb("ones", [128, 128], F32)
    ccs = sb("ccs", [128, 16], F32)
    modv = sb("modv", [128, L, 48, 2], F32)
    drv = sb("drv", [128, L, 6, 8, 2], F32)
    kkv = sb("kkv", [128, L, 2, 8], F32)
    kk2 = sb("kk2", [128, L, 2, 8], F32)
    carry = sb("carry", [128, 2, 8], F32)
    WG = sb("WG", [128, 4096], BF16)
    hbv = sb("hbv", [128, L, 4, 8], F32)
    kkh = sb("kkh", [128, L, 2, 8], F32)
    WA = sb("WA", [128, KC, INW], BF16)
    ARENA0 = apos["p"]
    WA_OFF = ARENA0 - 2 * KC * INW

    class Bufs:
        pass

    STG_N = 8

    def arena(tag, n, base, spec):
        apos["p"] = base
        b = Bufs()
        for (nm, kind, shape, dt, cnt) in spec:
            full = f"{tag}_{nm}"
            if kind == "ring":
                if full not in acache:
                    acache[full] = Ring(full, shape, dt, cnt)
                setattr(b, nm, acache[full])
            else:
                setattr(b, nm, sb(full, shape, dt))
        return b

    banks = [nc.alloc_psum_tensor(f"bank{i}", [128, NT], F32) for i in range(8)]
    rot = {"i": 0}

    def next_bank():
        j = rot["i"] % 6
        rot["i"] += 1
        return banks[j], ("bank", j)

    hrot = {"i": 0}

    def next_half():
        return next_bank()

    ST1, ST1K = banks[6], ("bank", 6)
    ST2, ST2K = banks[7], ("bank", 7)

    def A(o, i, f, r, w, scale=1.0, bias=0.0):
        fw.op(act, lambda h: h.activation(o, i, f, bias=bias, scale=scale), reads=r, writes=w)

    def TT(e, o, a, b_, op, r, w):
        fw.op(e, lambda h: h.tensor_tensor(o, a, b_, op), reads=r, writes=w)

    def STT(o, a, s_, b_, op0, op1, r, w):
        fw.op(dve, lambda h: h.scalar_tensor_tensor(o, a, s_, b_, op0, op1), reads=r, writes=w)

    def TS(e, o, a, s1, s2, op0, op1, r, w):
        fw.op(e, lambda h: h.tensor_scalar(o, a, s1, s2, op0, op1), reads=r, writes=w)

    def CP(e, o, i, r, w):
        fw.op(e, lambda h: h.tensor_copy(o, i), reads=r, writes=w)

    def RCP(o, i, r, w):
        fw.op(dve, lambda h: h.reciprocal(o, i), reads=r, writes=w)

    def MSET(o, v, w):
        fw.op(pool, lambda h: h.memset(o, v), writes=w)

    def SCAN(o, a, u, init, r, w):
        fw.op(dve, lambda h: h.tensor_tensor_scan(o, a, u, init, ALU.mult, ALU.add), reads=r, writes=w)

    def MM(o, lhsT, rhs, start, stop, r, w, sig=None):
        fw.op(pe, lambda h: h.matmul(o, lhsT, rhs, start=start, stop=stop), reads=r, writes=w,
              signal=(stop if sig is None else sig))

    def LD(o, i, r, w):
        fw.dma(sp, o, i, reads=r, writes=w)

    def STO(o, i, r, w):
        fw.dma(pool, o, i, reads=r, writes=w)

    def rows(ap2d, k):
        return ap2d[k * 128:(k + 1) * 128, :]

    NROWC = {"X": KC, "P": KC, "XR": KC, "GG": KC, "SG": 2 * KC, "REC": KC, "MA": KC, "Z": 2 * FC}

    def dk(name, lo, hi, rc=None):
        if rc is None:
            rc = range(NROWC[name[:-1]])
        elif isinstance(rc, int):
            rc = (rc,)
        return [(name, j, r) for j in range(lo // 256, (hi + 255) // 256) for r in rc]

    def barrier():
        toks = []
        for e in fw.engs:
            if e.count > 0:
                toks.append((e.sem, e.count))
            for (sem, val) in e.ring:
                if val > 0:
                    toks.append((sem, val))
        for e in fw.engs:
            waits = []
            for (sem, val) in toks:
                if sem is e.sem:
                    continue
                if e.known.get(id(sem), 0) < val:
                    e.known[id(sem)] = val
                    waits.append((sem, val))
            if waits:
                e.ops.append((waits, None, None))

    stg = arena("wstg", 0, SB_LIMIT - STG_N * 4096 - 64, [("r", "ring", [128, 1024], F32, STG_N)]).r
    cast_rr = {"i": 0}

    def load_weight(dst_ap_fn, src, nrows_chunks, ncols, wkey):
        for kc in range(nrows_chunks):
            for c0 in range(0, ncols, 1024):
                c1 = min(ncols, c0 + 1024)
                s_, sk = stg.next()
                LD(s_[:, :c1 - c0], src[kc * 128:(kc + 1) * 128, c0:c1], [], [sk])
                dst = dst_ap_fn(kc, c0, c1)
                j = cast_rr["i"] % 3
                cast_rr["i"] += 1
                if j != 1:
                    CP(dve, dst, s_[:, :c1 - c0], [sk], [wkey])
                else:
                    A(dst, s_[:, :c1 - c0], AF.Copy, [sk], [wkey])

    WAf = WA[:].rearrange("p k n -> p (k n)")

    LD(vecs[:], vecs_d.rearrange("l p n -> p l n"), [], ["vecs"])
    LD(ident[:], ident_d, [], ["ident"])
    LD(ccs[:], cc, [], ["ccs"])
    MSET(ones[:], 1.0, ["ones"])
    CP(dve, identb[:], ident[:], ["ident"], ["identb"])
    A(ccs[:], ccs[:], AF.Silu, ["ccs"], ["ccs"])

    def vcol(l, nm, k, w=1):
        o = _off[nm] + k
        return vecs[:, l, o:o + w]

    ccv = ccs[:].rearrange("p (k c) -> p k c", c=2)
    sbuf0 = arena("su", NT, ARENA0, [("wt", "ring", [128, KC, NT], F32, 2)])
    for l in range(L):
        for g in range(INW // NT):
            wt, wk = sbuf0.wt.next()
            LD(wt[:], w_ada[l][:, g * NT:(g + 1) * NT].rearrange("(kc p) n -> p kc n", p=128), [], [wk])
            for j in range(4):
                oc = g * 4 + j
                for kc in range(KC):
                    MM(ST1[:, oc * 2:oc * 2 + 2], wt[:, kc, j * 128:(j + 1) * 128], ccv[:, kc, :],
                       kc == 0, kc == KC - 1, [wk, "ccs"], [ST1K])
        bo = _off["b_ada"]
        for c in range(2):
            TT(dve, modv[:, l, :, c], ST1[:, 0:96].rearrange("p (o c) -> p o c", c=2)[:, :, c],
               vecs[:, l, bo:bo + 48], ALU.add, [ST1K, "vecs"], ["modv"])
        for c in range(2):
            for (di, gname, sc_o, sh_o) in [(0, "g_pre_mix", 8, 0), (3, "g_pre_ffn", 32, 24)]:
                go = _off[gname]
                TT(dve, drv[:, l, di, :, c], modv[:, l, sc_o:sc_o + 8, c], vecs[:, l, go:go + 8], ALU.mult,
                   ["modv", "vecs"], ["drv"])
                TT(dve, drv[:, l, di, :, c], drv[:, l, di, :, c], vecs[:, l, go:go + 8], ALU.add,
                   ["drv", "vecs"], ["drv"])
                CP(dve, drv[:, l, di + 1, :, c], modv[:, l, sh_o:sh_o + 8, c], ["modv"], ["drv"])
            for (di, gname, gt_o) in [(2, "g_post_mix", 16), (5, "g_post_ffn", 40)]:
                go = _off[gname]
                TT(dve, drv[:, l, di, :, c], modv[:, l, gt_o:gt_o + 8, c], vecs[:, l, go:go + 8], ALU.mult,
                   ["modv", "vecs"], ["drv"])
        for d in range(2):
            lo = _off["lam%d" % d]
            A(kkv[:, l, d, :], vecs[:, l, lo:lo + 8], AF.Exp, ["vecs"], ["kkv"], scale=-1.0)
            A(kkv[:, l, d, :], kkv[:, l, d, :], AF.Ln, ["kkv"], ["kkv"], bias=1.0)
            TS(dve, kkh[:, l, d, :], kkv[:, l, d, :], -4.0, None, ALU.mult, ALU.bypass, ["kkv"], ["kkh"])
            TS(dve, kkv[:, l, d, :], kkv[:, l, d, :], -8.0, None, ALU.mult, ALU.bypass, ["kkv", "kkh"], ["kkv"])
            for ri, nm_ in enumerate(("br", "bi")):
                bo_ = _off["%s%d" % (nm_, d)]
                TS(dve, hbv[:, l, d * 2 + ri, :], vecs[:, l, bo_:bo_ + 8], 0.5, None, ALU.mult, ALU.bypass,
                   ["vecs"], ["hbv"])
    barrier()
    stopped = {"v": STOP == "setup"}

    def DR(l, di, k, s):
        return drv[:, l, di, k, s.col:s.col + 1]

    def tiles_of(s, n):
        if s.T <= n:
            return [(0, s.T)]
        return [(i * n, n) for i in range(s.T // n)]

    def norm_pro1(b, l, s, t0, n, di, xsrc):
        x, xk = b.xt.next()
        LD(x[:, :, :n], xsrc[:, t0:t0 + n].rearrange("(kc p) n -> p kc n", p=128), dk("X" + s.nm, t0, t0 + n), [xk])
        for k in range(KC):
            q, qk = b.sq.next()
            A(q[:, :n], x[:, k, :n], AF.Square, [xk], [qk])
            MM(ST1[:, :n], ones[:], q[:, :n], k == 0, k == KC - 1, [qk, "ones"], [ST1K], sig=True)
        A(b.sd[:, :n], ST1[:, :n], AF.Sqrt, [ST1K], ["sd"], scale=1.0 / D, bias=EPS)
        RCP(b.rstd[:, :n], b.sd[:, :n], ["sd"], ["rstd"])
        return x, xk

    def norm_pro2(b, l, s, t0, n, di, x, xk):
        h, hk = b.hb.next()
        for k in range(KC):
            t, tk = b.tmp.next()
            TT(dve, t[:, :n], x[:, k, :n], b.rstd[:, :n], ALU.mult, [xk, "rstd"], [tk])
            A(h[:, k, :n], t[:, :n], AF.Identity, [tk, "drv"], [(hk, k)],
              scale=DR(l, di, k, s), bias=DR(l, di + 1, k, s))
        return h, hk

    pend = {}

    def post_oc(b, oc, bk, bkk, n):
        A(b.yb[:, oc, :n], bk[:, :n], AF.Copy, [bkk], [("yb", oc)])
        q, qk = b.sq.next()
        A(q[:, :n], bk[:, :n], AF.Square, [bkk], [qk])
        if oc > 0:
            pq, pqk = pend["q"]
            MM(ST1[:, :n], ones[:], pq[:, :n], oc == 1, False, [pqk, "ones"], [ST1K], sig=True)
        pend["q"] = (q, qk)
        if oc == KC - 1:
            MM(ST1[:, :n], ones[:], q[:, :n], False, True, [qk, "ones"], [ST1K], sig=True)

    def post_finish(b, l, s, t0, n, di, xsrc, xdst):
        A(b.sd[:, :n], ST1[:, :n], AF.Sqrt, [ST1K], ["sd"], scale=1.0 / D, bias=EPS)
        RCP(b.rstd[:, :n], b.sd[:, :n], ["sd"], ["rstd"])
        for k in range(KC):
            xl, xlk = b.ld32.next()
            LD(xl[:, :n], rows(xsrc, k)[:, t0:t0 + n], dk("X" + s.nm, t0, t0 + n, k), [xlk])
            t, tk = b.tmp.next()
            STT(t[:, :n], b.yb[:, k, :n], DR(l, di, k, s), b.rstd[:, :n], ALU.mult, ALU.mult,
                [("yb", k), "rstd", "drv"], [tk])
            o, ok = b.st32.next()
            TT(pool, o[:, :n], t[:, :n], xl[:, :n], ALU.add, [tk, xlk], [ok])
            STO(rows(xdst, k)[:, t0:t0 + n], o[:, :n], [ok], dk("X" + s.nm, t0, t0 + n, k))

    def phase1(l, s, xsrc, full):
        N = NT
        b = arena("p1", N, ARENA0, [
            ("xt", "ring", [128, KC, N], F32, 1), ("sq", "ring", [128, N], F32, 3), ("tmp", "ring", [128, N], F32, 3),
            ("hb", "ring", [128, KC, N], BF16, 2), ("st32", "ring", [128, N], F32, 3), ("st16", "ring", [128, N], BF16, 2),
            ("sgr", "ring", [128, N], F32, 2), ("sd", "t", [128, N], F32, 1), ("rstd", "t", [128, N], F32, 1)])
        if full:
            order = []
            for j in range(8):
                order += [8 + j, j]
            order += list(range(16, 24)) + list(range(32, 48)) + list(range(24, 32))
        else:
            order = list(range(16, 24))
        tl = tiles_of(s, N)
        st = norm_pro1(b, l, s, tl[0][0], tl[0][1], 0, xsrc)
        nxt_h = norm_pro2(b, l, s, tl[0][0], tl[0][1], 0, *st)
        for ti, (t0, n) in enumerate(tl):
            h, hk = nxt_h
            nx = tl[ti + 1] if ti + 1 < len(tl) else None
            sgk_of = {}
            for oi, oc in enumerate(order):
                if nx and oi == len(order) // 4:
                    st = norm_pro1(b, l, s, nx[0], nx[1], 0, xsrc)
                if nx and oi == (2 * len(order)) // 3:
                    nxt_h = norm_pro2(b, l, s, nx[0], nx[1], 0, *st)
                bk, bkk = next_bank()
                for kc in range(KC):
                    MM(bk[:, :n], WA[:, kc, oc * 128:(oc + 1) * 128], h[:, kc, :n], kc == 0, kc == KC - 1,
                       [(hk, kc), "WA"], [bkk])
                if 8 <= oc < 16:
                    g, gk = b.sgr.next()
                    A(g[:, :n], bk[:, :n], AF.Sigmoid, [bkk], [gk])
                    sgk_of[oc - 8] = (g, gk)
                elif oc < 8:
                    g, gk = sgk_of[oc]
                    o, ok = b.st16.next()
                    TT(dve, o[:, :n], bk[:, :n], g[:, :n], ALU.mult, [bkk, gk], [ok])
                    STO(rows(s.P, oc)[:, t0:t0 + n], o[:, :n], [ok], dk("P" + s.nm, t0, t0 + n, oc))
                elif oc < 24:
                    o, ok = b.st32.next()
                    A(o[:, :n], bk[:, :n], AF.Copy, [bkk], [ok])
                    STO(rows(s.XR, oc - 16)[:, t0:t0 + n], o[:, :n], [ok], dk("XR" + s.nm, t0, t0 + n, oc - 16))
                elif oc < 32:
                    o, ok = b.st32.next()
                    A(o[:, :n], bk[:, :n], AF.Gelu, [bkk], [ok])
                    STO(rows(s.GG, oc - 24)[:, t0:t0 + n], o[:, :n], [ok], dk("GG" + s.nm, t0, t0 + n, oc - 24))
                else:
                    o, ok = b.st32.next()
                    A(o[:, :n], bk[:, :n], AF.Sigmoid, [bkk], [ok])
                    STO(rows(s.SG, oc - 32)[:, t0:t0 + n], o[:, :n], [ok], dk("SG" + s.nm, t0, t0 + n, oc - 32))

    def gate_w(d, ri, k):
        o = ((d * 2 + ri) * 8 + k) * 128
        return WG[:, o:o + 128]

    def lru_spec(N):
        return [("xw", "t", [128, KC, N + 4], F32, 1), ("xc", "t", [128, KC, N], F32, 1),
                ("xcb", "t", [128, KC, N], BF16, 1), ("ra", "t", [128, KC, N], F32, 1),
                ("iu", "t", [128, KC, N], F32, 1), ("a2", "t", [128, KC, N], F32, 1),
                ("rec", "t", [128, KC, N], F32, 1)]

    def lru_tile(b, l, s, t0, n, d, part="AB"):
        allk = lambda nm: [(nm, k) for k in range(KC)]
        if "A" in part:
            lru_part_a(b, l, s, t0, n, d, allk)
        if "B" in part:
            lru_part_b(b, l, s, t0, n, d, allk)

    def lru_part_a(b, l, s, t0, n, d, allk):
        lo, hi = t0 - 1, t0 + n + 2
        clo, chi = max(lo, 0), min(hi, s.T)
        if clo > lo or chi < hi:
            MSET(b.xw[:, :, :n + 3], 0.0, ["xw"])
        LD(b.xw[:, :, clo - lo:chi - lo], s.XR[:, clo:chi].rearrange("(kc p) n -> p kc n", p=128),
           dk("XR" + s.nm, clo, chi), ["xw"])
        for k in range(KC):
            lw = _off["lru_w"] + k * 4
            TS(dve, b.xc[:, k, :n], b.xw[:, k, 0:n], vecs[:, l, lw:lw + 1], vcol(l, "lru_b", k), ALU.mult, ALU.add,
               ["xw", "vecs"], [("xc", k)])
            for j in range(1, 4):
                STT(b.xc[:, k, :n], b.xw[:, k, j:j + n], vecs[:, l, lw + j:lw + j + 1], b.xc[:, k, :n],
                    ALU.mult, ALU.add, ["xw", ("xc", k), "vecs"], [("xc", k)])
        if DBG <= 7:
            return
        CP(dve, b.xcb[:, :, :n], b.xc[:, :, :n], allk("xc"), ["xcb"])
        if DBG <= 8:
            return
        for k in range(KC):
            for ri, dst in ((0, b.ra), (1, b.iu)):
                hb_, hk_ = next_half()
                MM(hb_[:, :n], gate_w(d, ri, k), b.xcb[:, k, :n], True, True, ["xcb", "WG"], [hk_])
                A(dst[:, k, :n], hb_[:, :n], AF.Tanh, [hk_, "hbv"], [("ra" if ri == 0 else "iu", k)],
                  scale=0.5, bias=hbv[:, l, d * 2 + ri, k:k + 1])
        if DBG <= 9:
            return
        for k in range(KC):
            TS(dve, b.ra[:, k, :n], b.ra[:, k, :n], kkh[:, l, d, k:k + 1], kkh[:, l, d, k:k + 1], ALU.mult, ALU.add,
               [("ra", k), "kkh"], [("ra", k)])
        A(b.a2[:, :, :n], b.ra[:, :, :n], AF.Exp, allk("ra"), allk("a2"), scale=2.0)
        A(b.ra[:, :, :n], b.ra[:, :, :n], AF.Exp, allk("ra"), allk("ra"))
        if DBG <= 9.3:
            return
        A(b.a2[:, :, :n], b.a2[:, :, :n], AF.Sqrt, allk("a2"), allk("a2"), scale=-0.25, bias=0.25)

    def lru_part_b(b, l, s, t0, n, d, allk):
        STT(b.iu[:, :, :n], b.iu[:, :, :n], 1.0, b.xc[:, :, :n], ALU.add, ALU.mult, allk("iu") + allk("xc"), allk("iu"))
        TT(dve, b.iu[:, :, :n], b.iu[:, :, :n], b.a2[:, :, :n], ALU.mult, allk("iu") + allk("a2"), allk("iu"))
        if DBG <= 9.6:
            return
        for k in range(KC):
            if d == 0:
                SCAN(b.rec[:, k, :n], b.ra[:, k, :n], b.iu[:, k, :n], carry[:, d, k:k + 1],
                     [("ra", k), ("iu", k), ("carry", d)], [("rec", k)])
            else:
                SCAN(b.rec[:, k, n - 1::-1], b.ra[:, k, n - 1::-1], b.iu[:, k, n - 1::-1], carry[:, d, k:k + 1],
                     [("ra", k), ("iu", k), ("carry", d)], [("rec", k)])
        edge = n - 1 if d == 0 else 0
        CP(pool, carry[:, d, :], b.rec[:, :, edge], allk("rec"), [("carry", d)])

    N2A = 256
    WB_PC = 0
    WB_DG = 8192
    WB_SZ = 8192 + 31744

    def arena2a():
        return arena("p2a", N2A, WA_OFF, [("WB", "t", [128, WB_SZ], BF16, 1)] + lru_spec(N2A) + [
            ("pw", "t", [128, KC, N2A + 32], BF16, 1), ("yb", "t", [128, KC, N2A], F32, 1),
            ("sq", "ring", [128, N2A], F32, 3), ("tmp", "ring", [128, N2A], F32, 3),
            ("hb", "t", [128, KC, N2A], BF16, 1), ("ld32", "ring", [128, N2A], F32, 3),
            ("st32", "ring", [128, N2A], F32, 3), ("sd", "t", [128, N2A], F32, 1), ("rstd", "t", [128, N2A], F32, 1),
            ("mean", "t", [128, N2A], F32, 1), ("nmr", "t", [128, N2A], F32, 1)])

    def phase2a(l, s, full):
        b = arena2a()
        WB = b.WB
        allk = lambda nm: [(nm, k) for k in range(KC)]

        def lru(t0, n, part="AB"):
            lru_tile(b, l, s, t0, n, 0, part)
            if DBG <= 10 or "B" not in part:
                return
            STO(s.REC[:, t0:t0 + n].rearrange("(kc p) n -> p kc n", p=128), b.rec[:, :, :n], allk("rec"),
                dk("REC" + s.nm, t0, t0 + n))

        def conformer(t0, n, part):
            if part == 1:
                conf1(t0, n)
            else:
                conf2(t0, n)

        def conf1(t0, n):
            lo, hi = t0 - 15, t0 + n + 15
            clo, chi = max(lo, 0), min(hi, s.T)
            if clo > lo or chi < hi:
                MSET(b.pw[:, :, :n + 30], 0.0, ["pw"])
            LD(b.pw[:, :, clo - lo:chi - lo], s.P[:, clo:chi].rearrange("(kc p) n -> p kc n", p=128),
               dk("P" + s.nm, clo, chi), ["pw"])
            if DBG <= 12.1:
                return
            for k in range(KC):
                bk, bkk = next_half()
                for j in range(CONV_K):
                    o = WB_DG + (k * CONV_K + j) * 128
                    MM(bk[:, :n], WB[:, o:o + 128], b.pw[:, k, j:j + n], j == 0, j == CONV_K - 1, ["pw", ("WBd", k, j)], [bkk])
                if DBG <= 12.2:
                    continue
                A(b.yb[:, k, :n], bk[:, :n], AF.Copy, [bkk], [("yb", k)])
                q, qk = b.sq.next()
                A(q[:, :n], bk[:, :n], AF.Square, [bkk], [qk])
                if k > 0:
                    pq, pqk = pend["cq"]
                    MM(ST1[:, :n], ones[:], b.yb[:, k - 1, :n], k == 1, False, [("yb", k - 1), "ones"], [ST1K], sig=True)
                    MM(ST2[:, :n], ones[:], pq[:, :n], k == 1, False, [pqk, "ones"], [ST2K], sig=True)
                pend["cq"] = (q, qk)
                if k == KC - 1:
                    MM(ST1[:, :n], ones[:], b.yb[:, k, :n], False, True, [("yb", k), "ones"], [ST1K], sig=True)
                    MM(ST2[:, :n], ones[:], q[:, :n], False, True, [qk, "ones"], [ST2K], sig=True)

        def conf2(t0, n):
            A(b.mean[:, :n], ST1[:, :n], AF.Copy, [ST1K], ["mean"], scale=1.0 / D)
            A(b.nmr[:, :n], ST1[:, :n], AF.Square, [ST1K], ["nmr"], scale=1.0 / D)
            STT(b.sd[:, :n], ST2[:, :n], 1.0 / D, b.nmr[:, :n], ALU.mult, ALU.subtract, [ST2K, "nmr"], ["sd"])
            A(b.sd[:, :n], b.sd[:, :n], AF.Sqrt, ["sd"], ["sd"], bias=EPS)
            RCP(b.rstd[:, :n], b.sd[:, :n], ["sd"], ["rstd"])
            STT(b.nmr[:, :n], b.mean[:, :n], -1.0, b.rstd[:, :n], ALU.mult, ALU.mult, ["mean", "rstd", "nmr"], ["nmr"])
            for k in range(KC):
                t, tk = b.tmp.next()
                TT(dve, t[:, :n], b.yb[:, k, :n], b.rstd[:, :n], ALU.mult, [("yb", k), "rstd"], [tk])
                TT(pool, t[:, :n], t[:, :n], b.nmr[:, :n], ALU.add, [tk, "nmr"], [tk])
                A(b.hb[:, k, :n], t[:, :n], AF.Silu, [tk, "vecs"], [("hb", k)],
                  scale=vcol(l, "ln_g", k), bias=vcol(l, "ln_b", k))
            if DBG <= 12.4:
                return
            for oc in range(KC):
                sg_, sgk_ = b.ld32.next()
                LD(sg_[:, :n], rows(s.SG, oc)[:, t0:t0 + n], dk("SG" + s.nm, t0, t0 + n, oc), [sgk_])
                bk, bkk = next_half()
                for kc in range(KC):
                    o = WB_PC + kc * 1024 + oc * 128
                    MM(bk[:, :n], WB[:, o:o + 128], b.hb[:, kc, :n], kc == 0, kc == KC - 1, [("hb", kc), "WBp"], [bkk])
                if DBG <= 12.5:
                    continue
                o_, ok_ = b.st32.next()
                TT(dve, o_[:, :n], bk[:, :n], sg_[:, :n], ALU.mult, [bkk, sgk_], [ok_])
                if DBG <= 12.6:
                    continue
                STO(rows(s.MA, oc)[:, t0:t0 + n], o_[:, :n], [ok_], dk("MA" + s.nm, t0, t0 + n, oc))

        tl = tiles_of(s, N2A)
        lru(*tl[0])
        for i, (t0, n) in enumerate(tl):
            if full:
                conformer(t0, n, 1)
            if i + 1 < len(tl):
                lru(*tl[i + 1], part="A")
            if full:
                conformer(t0, n, 2)
            if i + 1 < len(tl):
                lru(*tl[i + 1], part="B")

    N2B = 256

    def phase2b(l, s, full, xsrc, xdst):
        N = N2B
        b = arena("p2b", N, WA_OFF + 2 * 2 * KC * D, lru_spec(N) + [
            ("rf", "t", [128, KC, N], F32, 1), ("gg", "t", [128, KC, N], F32, 1),
            ("qb", "ring", [128, KC, N], BF16, 2), ("hb", "t", [128, KC, N], BF16, 1),
            ("ld32", "ring", [128, N], F32, 6), ("tmp", "ring", [128, N], F32, 3), ("yb", "t", [128, KC, N], F32, 1),
            ("sq", "ring", [128, N], F32, 3), ("st32", "ring", [128, N], F32, 3),
            ("sd", "t", [128, N], F32, 1), ("rstd", "t", [128, N], F32, 1)])
        allk = lambda nm: [(nm, k) for k in range(KC)]

        def front(t0, n, part="AB"):
            lru_tile(b, l, s, t0, n, 1, part)
            if "B" not in part:
                return None
            if not full:
                return None
            LD(b.rf[:, :, :n], s.REC[:, t0:t0 + n].rearrange("(kc p) n -> p kc n", p=128),
               dk("REC" + s.nm, t0, t0 + n), ["rf"])
            LD(b.gg[:, :, :n], s.GG[:, t0:t0 + n].rearrange("(kc p) n -> p kc n", p=128),
               dk("GG" + s.nm, t0, t0 + n), ["gg"])
            TT(dve, b.rec[:, :, :n], b.rec[:, :, :n], b.rf[:, :, :n], ALU.add, allk("rec") + ["rf"], allk("rec"))
            q_, qk_ = b.qb.next()
            TT(dve, q_[:, :, :n], b.rec[:, :, :n], b.gg[:, :, :n], ALU.mult, allk("rec") + ["gg"], [qk_])
            return q_, qk_

        def back(t0, n, q_, qk_):
            for oc in range(KC):
                sg_, sgk_ = b.ld32.next()
                LD(sg_[:, :n], rows(s.SG, 8 + oc)[:, t0:t0 + n], dk("SG" + s.nm, t0, t0 + n, 8 + oc), [sgk_])
                ma, mak = b.ld32.next()
                LD(ma[:, :n], rows(s.MA, oc)[:, t0:t0 + n], dk("MA" + s.nm, t0, t0 + n, oc), [mak])
                bk, bkk = next_half()
                for kc in range(KC):
                    MM(bk[:, :n], WAf[:, kc * D + oc * 128:kc * D + (oc + 1) * 128], q_[:, kc, :n], kc == 0,
                       kc == KC - 1, [qk_, "WA"], [bkk])
                t, tk = b.tmp.next()
                TT(dve, t[:, :n], bk[:, :n], sg_[:, :n], ALU.mult, [bkk, sgk_], [tk])
                TT(pool, b.hb[:, oc, :n], t[:, :n], ma[:, :n], ALU.add, [tk, mak], [("hb", oc)])
            for oc in range(KC):
                bk, bkk = next_half()
                for kc in range(KC):
                    MM(bk[:, :n], WAf[:, (KC + kc) * D + oc * 128:(KC + kc) * D + (oc + 1) * 128], b.hb[:, kc, :n],
                       kc == 0, kc == KC - 1, [("hb", kc), "WA"], [bkk])
                post_oc(b, oc, bk, bkk, n)

        tl = list(reversed(tiles_of(s, N)))
        cur = front(*tl[0])
        for i, (t0, n) in enumerate(tl):
            if full:
                back(t0, n, *cur)
            nxt = None
            if i + 1 < len(tl):
                front(*tl[i + 1], part="A")
                nxt = front(*tl[i + 1], part="B")
            if full:
                post_finish(b, l, s, t0, n, 2, xsrc, xdst)
            cur = nxt

    def phase3a(l, s, xsrc):
        N = NT
        b = arena("p3a", N, ARENA0, [
            ("xt", "ring", [128, KC, N], F32, 1), ("sq", "ring", [128, N], F32, 3), ("tmp", "ring", [128, N], F32, 3),
            ("hb", "ring", [128, KC, N], BF16, 2), ("st16", "ring", [128, N], BF16, 6),
            ("sd", "t", [128, N], F32, 1), ("rstd", "t", [128, N], F32, 1)])
        ev = 0
        tl = tiles_of(s, N)
        st = norm_pro1(b, l, s, tl[0][0], tl[0][1], 3, xsrc)
        nxt_h = norm_pro2(b, l, s, tl[0][0], tl[0][1], 3, *st)
        for ti, (t0, n) in enumerate(tl):
            h, hk = nxt_h
            nx = tl[ti + 1] if ti + 1 < len(tl) else None
            for oc in range(2 * FC):
                if nx and oc == FC // 2:
                    st = norm_pro1(b, l, s, nx[0], nx[1], 3, xsrc)
                if nx and oc == (4 * FC) // 3:
                    nxt_h = norm_pro2(b, l, s, nx[0], nx[1], 3, *st)
                bk, bkk = next_bank()
                for kc in range(KC):
                    MM(bk[:, :n], WA[:, kc, oc * 128:(oc + 1) * 128], h[:, kc, :n], kc == 0, kc == KC - 1,
                       [(hk, kc), "WA"], [bkk])
                o, ok = b.st16.next()
                if ev % 2 == 0:
                    A(o[:, :n], bk[:, :n], AF.Copy, [bkk], [ok])
                else:
                    CP(dve, o[:, :n], bk[:, :n], [bkk], [ok])
                ev += 1
                STO(rows(s.Z, oc)[:, t0:t0 + n], o[:, :n], [ok], dk("Z" + s.nm, t0, t0 + n, oc))

    def WDN(j, oc):
        o = j * D + oc * 128
        return WAf[:, o:o + 128]

    N3B = 256

    def arena3b():
        N = N3B
        return arena("p3b", N, WA_OFF + 2 * FC * D, [
            ("DG", "t", [128, 2 * FC * 9 * 128], BF16, 1),
            ("zw", "ring", [128, N + 2 * GW], BF16, 12), ("sl", "ring", [128, N], F32, 2),
            ("fb", "ring", [128, FC, N], BF16, 2), ("yb", "t", [128, KC, N], F32, 1), ("sq", "ring", [128, N], F32, 3),
            ("tmp", "ring", [128, N], F32, 2), ("ld32", "ring", [128, N], F32, 2), ("st32", "ring", [128, N], F32, 2),
            ("sd", "t", [128, N], F32, 1), ("rstd", "t", [128, N], F32, 1)])

    def phase3b(l, s, xsrc, xdst):
        b = arena3b()
        DG = b.DG

        def dg(ch, tap):
            o = (ch * 9 + tap) * 128
            return DG[:, o:o + 128]

        for (t0, n) in tiles_of(s, N3B):
            f_, fk_ = b.fb.next()
            for j in range(FC):
                bks = {}
                for half in (1, 0):
                    ch = half * FC + j
                    z, zk = b.zw.next()
                    bk, bkk = next_bank()
                    bks[half] = (bk, bkk)
                    if s.grid:
                        lo, hi = t0 - GW, t0 + n + GW
                        clo, chi = max(lo, 0), min(hi, s.T)
                        if clo > lo or chi < hi:
                            MSET(z[:, :n + 2 * GW], 0.0, [zk])
                        LD(z[:, clo - lo:chi - lo], rows(s.Z, ch)[:, clo:chi], dk("Z" + s.nm, clo, chi, ch), [zk])
                        nr = n // GW
                        zv = z[:, :n + 2 * GW].rearrange("p (r c) -> p r c", c=GW)
                        pv = bk[:, :n].rearrange("p (r c) -> p r c", c=GW)
                        taps = [(0, 0)] + [(dy, dx) for dy in (-1, 0, 1) for dx in (-1, 0, 1) if (dy, dx) != (0, 0)]
                        for i, (dy, dx) in enumerate(taps):
                            c0, c1 = max(0, -dx), GW - max(0, dx)
                            MM(pv[:, :, c0:c1], dg(ch, (dy + 1) * 3 + dx + 1),
                               zv[:, 1 + dy:1 + dy + nr, c0 + dx:c1 + dx], i == 0, i == len(taps) - 1,
                               [zk, ("DG", ch * 9 + (dy + 1) * 3 + dx + 1)], [bkk])
                    else:
                        MSET(z[:, :n + 2], 0.0, [zk])
                        LD(z[:, 1:1 + n], rows(s.Z, ch)[:, t0:t0 + n], dk("Z" + s.nm, t0, t0 + n, ch), [zk])
                        for i, dx in enumerate((0, -1, 1)):
                            MM(bk[:, :n], dg(ch, 3 + dx + 1), z[:, 1 + dx:1 + dx + n], i == 0, i == 2, [zk, ("DG", ch * 9 + 3 + dx + 1)], [bkk])
                (bv, bvk), (bg, bgk) = bks[0], bks[1]
                sl, slk = b.sl.next()
                A(sl[:, :n], bg[:, :n], AF.Silu, [bgk], [slk])
                TT(dve, f_[:, j, :n], bv[:, :n], sl[:, :n], ALU.mult, [bvk, slk], [(fk_, j)])
            for oc in range(KC):
                bk, bkk = next_bank()
                for j in range(FC):
                    MM(bk[:, :n], WDN(j, oc), f_[:, j, :n], j == 0, j == FC - 1, [(fk_, j), "WA"], [bkk])
                post_oc(b, oc, bk, bkk, n)
            post_finish(b, l, s, t0, n, 5, xsrc, xdst)

    def load_p1(l):
        load_weight(lambda kc, c0, c1: WA[:, kc, c0:c1], w_in[l], KC, INW, "WA")

    def load_p2a(l):
        WB = arena2a().WB
        load_weight(lambda kc, c0, c1: WB[:, WB_PC + kc * 1024 + c0:WB_PC + kc * 1024 + c1], w_pc[l], KC, D, "WBp")
        dwc = _off["dw_conv"]
        for k in range(KC):
            for j in range(CONV_K):
                o = WB_DG + (k * CONV_K + j) * 128
                wi = dwc + k * CONV_K + j
                if (k * CONV_K + j) % 3 != 2:
                    TS(dve, WB[:, o:o + 128], identb[:], vecs[:, l, wi:wi + 1], None, ALU.mult, ALU.bypass,
                       ["identb", "vecs"], [("WBd", k, j)])
                else:
                    A(WB[:, o:o + 128], identb[:], AF.Copy, ["identb", "vecs"], [("WBd", k, j)],
                      scale=vecs[:, l, wi:wi + 1])
        for d in range(2):
            for ri, wsrc in enumerate((w_rg, w_ig)):
                s_, sk = stg.next()
                MSET(s_[:, :1024], 0.0, [sk])
                sv = s_[:, :1024].rearrange("p (k j) -> p k j", j=128)
                src = wsrc[l, d].rearrange("(k two) i j -> two i k j", two=2)
                LD(sv[0:64, :, 0:64], src[0], [], [sk])
                LD(sv[64:128, :, 64:128], src[1], [], [sk])
                o = (d * 2 + ri) * 1024
                CP(dve, WG[:, o:o + 1024], s_[:, :1024], [sk], ["WG"])

    def load_p2b(l):
        load_weight(lambda kc, c0, c1: WAf[:, kc * D + c0:kc * D + c1], w_pl[l], KC, D, "WA")
        load_weight(lambda kc, c0, c1: WAf[:, (KC + kc) * D + c0:(KC + kc) * D + c1], w_o[l], KC, D, "WA")

    def load_p3a(l):
        load_weight(lambda kc, c0, c1: WA[:, kc, c0:c1], w_up[l], KC, 2 * FFN, "WA")

    def load_p3b(l):
        load_weight(lambda j, c0, c1: WAf[:, j * D + c0:j * D + c1], w_dn[l], FC, D, "WA")
        DG = arena3b().DG
        dwo = _off["dw_ffn"]
        for i in range(2 * FC * 9):
            if i % 3 != 2:
                TS(dve, DG[:, i * 128:(i + 1) * 128], identb[:], vecs[:, l, dwo + i:dwo + i + 1], None, ALU.mult,
                   ALU.bypass, ["identb", "vecs"], [("DG", i)])
            else:
                A(DG[:, i * 128:(i + 1) * 128], identb[:], AF.Copy, ["identb", "vecs"], [("DG", i)],
                  scale=vecs[:, l, dwo + i:dwo + i + 1])

    ckeys = [("carry", d) for d in range(2)]
    def stop_at(tag):
        if STOP == tag:
            stopped["v"] = True
        return stopped["v"]

    for l in range(NLAYERS):
        if stopped["v"]:
            break
        last = (l == L - 1)
        xs_l = xin if l == 0 else out
        xs_c = ctxin if l == 0 else ctx.xres
        load_p1(l)
        barrier()
        if stop_at("w1"):
            break
        phase1(l, ctx, xs_c, not last)
        if stop_at("p1c"):
            break
        phase1(l, lat, xs_l, True)
        barrier()
        if stop_at("p1"):
            break
        load_p2a(l)
        fw.op(pool, lambda h: h.memset(carry[:], 0.0), reads=ckeys, writes=ckeys)
        barrier()
        if stop_at("w2a"):
            break
        phase2a(l, ctx, not last)
        if stop_at("p2ac"):
            break
        phase2a(l, lat, True)
        barrier()
        if stop_at("p2a"):
            break
        load_p2b(l)
        barrier()
        phase2b(l, ctx, not last, xs_c, ctx.xres)
        if stop_at("p2bc"):
            break
        phase2b(l, lat, True, xs_l, out)
        barrier()
        if stop_at("p2b"):
            break
        load_p3a(l)
        barrier()
        if not last:
            phase3a(l, ctx, ctx.xres)
        phase3a(l, lat, out)
        barrier()
        if stop_at("p3a"):
            break
        load_p3b(l)
        barrier()
        if not last:
            phase3b(l, ctx, ctx.xres, ctx.xres)
        if stop_at("p3bc"):
            break
        phase3b(l, lat, out, out)
        barrier()

    fw.finish()
    fw.emit()
    return nc, fw


def _pc(v):
    v = np.asarray(v, np.float32)
    return np.ascontiguousarray(v.reshape(-1, 128).T)


def _pack_vecs(inp, l):
    cols = [None] * 0
    parts = []
    parts.append(_pc(inp["g_pre_mix"][l]))
    parts.append(_pc(inp["g_post_mix"][l]))
    parts.append(_pc(inp["g_pre_ffn"][l]))
    parts.append(_pc(inp["g_post_ffn"][l]))
    parts.append(_pc(inp["ln_conv_g"][l]))
    parts.append(_pc(inp["ln_conv_b"][l]))
    parts.append(_pc(inp["lru_conv_b"][l]))
    parts.append(_pc(inp["b_rgate"][l, 0]))
    parts.append(_pc(inp["b_rgate"][l, 1]))
    parts.append(_pc(inp["b_igate"][l, 0]))
    parts.append(_pc(inp["b_igate"][l, 1]))
    parts.append(_pc(inp["lru_lambda"][l, 0]))
    parts.append(_pc(inp["lru_lambda"][l, 1]))
    parts.append(_pc(inp["b_ada"][l]))
    w = np.asarray(inp["lru_conv_w"][l], np.float32)
    parts.append(np.ascontiguousarray(w.reshape(4, 8, 128).transpose(2, 1, 0)).reshape(128, 32))
    w = np.asarray(inp["dw_conv"][l], np.float32)
    parts.append(np.ascontiguousarray(w.reshape(CONV_K, 8, 128).transpose(2, 1, 0)).reshape(128, 8 * CONV_K))
    w = np.asarray(inp["dw_ffn"][l], np.float32)
    parts.append(np.ascontiguousarray(w.reshape(9, 40, 128).transpose(2, 1, 0)).reshape(128, 360))
    v = np.concatenate(parts, axis=1)
    assert v.shape == (128, NV), v.shape
    return v


_CACHE = {}


def kernel(**inputs):
    inp = {k: np.asarray(v) for k, v in inputs.items()}
    if "nc" not in _CACHE:
        _CACHE["nc"] = build_program()[0]
    nc = _CACHE["nc"]
    B = inp["x"].shape[0]
    vecs = np.stack([_pack_vecs(inp, l) for l in range(L)], axis=0)
    ident = np.eye(128, dtype=np.float32)
    cctx = _pc(inp["c_ctx"])
    shared = {
        "vecs": vecs, "ident": ident,
        "w_ada": np.ascontiguousarray(inp["w_ada"], np.float32),
        "w_in": np.ascontiguousarray(inp["w_in"], np.float32),
        "w_proj_conv": np.ascontiguousarray(inp["w_proj_conv"], np.float32),
        "w_rgate": np.ascontiguousarray(inp["w_rgate"], np.float32),
        "w_igate": np.ascontiguousarray(inp["w_igate"], np.float32),
        "w_proj_lru": np.ascontiguousarray(inp["w_proj_lru"], np.float32),
        "w_out": np.ascontiguousarray(inp["w_out"], np.float32),
        "w_up": np.ascontiguousarray(inp["w_up"], np.float32),
        "w_down": np.ascontiguousarray(inp["w_down"], np.float32),
    }
    in_maps = []
    for b in range(B):
        m = dict(shared)
        m["xin"] = np.ascontiguousarray(inp["x"][b].T, np.float32)
        m["ctxin"] = np.ascontiguousarray(inp["ctx"][b].T, np.float32)
        cb = _pc(inp["c"][b])
        m["cc"] = np.ascontiguousarray(np.stack([cb, cctx], axis=2).reshape(128, 16))
        in_maps.append(m)
    res = run_bass_kernel_spmd(nc, in_maps, core_ids=list(range(B)))
    if DEBUG_OUT:
        _DBG_RES.update(res.results[0])
    outp = np.stack([np.ascontiguousarray(r["out"].T) for r in res.results], axis=0)
    return outp.astype(np.float32)
```

```python
import numpy as np
import concourse.bass as bass
import concourse.mybir as mybir
from concourse.bass_utils import run_bass_kernel_spmd

F32 = mybir.dt.float32
BF16 = mybir.dt.bfloat16
AF = mybir.ActivationFunctionType
ALU = mybir.AluOpType

L = 4
D = 1024
T = 4096
TC = 256
KC = 8
NT = 512
GW = 64
FFN = 2560
FC = 20
EPS = 1e-6
CONV_K = 31
INW = 6144

SEM_ROT = 24000
DMA_RING = 12

_off = {}
_o = 0
for _nm, _w in [("g_pre_mix", 8), ("g_post_mix", 8), ("g_pre_ffn", 8), ("g_post_ffn", 8),
                ("ln_g", 8), ("ln_b", 8), ("lru_b", 8), ("br0", 8), ("br1", 8), ("bi0", 8), ("bi1", 8),
                ("lam0", 8), ("lam1", 8), ("b_ada", 48), ("lru_w", 32), ("dw_conv", 248), ("dw_ffn", 360)]:
    _off[_nm] = _o
    _o += _w
NV = _o


class _Eng:
    def __init__(self, fw, name, handle, is_pe=False):
        self.fw = fw
        self.name = name
        self.h = handle
        self.is_pe = is_pe
        self.ops = []
        self.sem = None
        self.count = 0
        self.known = {}
        self.ring = []
        self.dma_i = 0

    def new_sem(self):
        self.sem = self.fw.alloc_sem(self.name)
        self.count = 0


class FW:
    def __init__(self, nc):
        self.nc = nc
        self.nsem = 0
        self.state = {}
        self.pe = _Eng(self, "pe", nc.tensor, is_pe=True)
        self.act = _Eng(self, "act", nc.scalar)
        self.dve = _Eng(self, "dve", nc.vector)
        self.pool = _Eng(self, "pool", nc.gpsimd)
        self.sp = _Eng(self, "sp", nc.sync)
        self.engs = [self.pe, self.act, self.dve, self.pool, self.sp]
        for e in self.engs:
            e.new_sem()
        self.n_ops = 0

    def alloc_sem(self, name):
        s = self.nc.alloc_semaphore(name=f"s_{name}_{self.nsem}")
        self.nsem += 1
        return s

    def _deps(self, eng, reads, writes):
        toks = []
        for k in reads:
            st = self.state.get(k)
            if st and st[0] is not None:
                toks.append(st[0])
        for k in writes:
            st = self.state.get(k)
            if st:
                if st[0] is not None:
                    toks.append(st[0])
                toks.extend(st[1])
        waits = []
        for (sem, val) in toks:
            if sem is eng.sem:
                if eng.is_pe:
                    continue
                if val < eng.count - 1:
                    continue
            kv = eng.known.get(id(sem), 0)
            if kv >= val:
                continue
            eng.known[id(sem)] = val
            waits.append((sem, val))
        return waits

    def _commit(self, tok, reads, writes):
        for k in reads:
            st = self.state.setdefault(k, [None, []])
            st[1].append(tok)
            if len(st[1]) > 64:
                del st[1][:32]
        for k in writes:
            self.state[k] = [tok, []]

    def op(self, eng, fn, reads=(), writes=(), signal=True):
        waits = self._deps(eng, reads, writes)
        if signal:
            if eng.count >= SEM_ROT:
                eng.new_sem()
            eng.count += 1
            tok = (eng.sem, eng.count)
            inc = (eng.sem, 1)
        else:
            tok = (eng.sem, eng.count + 1)
            inc = None
        eng.ops.append((waits, fn, inc))
        self._commit(tok, reads, writes)
        self.n_ops += 1
        return tok

    def dma(self, eng, out, in_, reads=(), writes=()):
        if not eng.ring:
            eng.ring = [[self.alloc_sem(eng.name + "_dma"), 0] for _ in range(DMA_RING)]
        slot = eng.ring[eng.dma_i % DMA_RING]
        eng.dma_i += 1
        if slot[1] >= SEM_ROT:
            extra = [(slot[0], slot[1])]
            slot[0] = self.alloc_sem(eng.name + "_dma")
            slot[1] = 0
        else:
            extra = [(slot[0], slot[1])] if slot[1] > 0 else []
        waits = self._deps(eng, reads, writes)
        for (sem, val) in extra:
            if eng.known.get(id(sem), 0) < val:
                eng.known[id(sem)] = val
                waits.append((sem, val))
        slot[1] += 16
        tok = (slot[0], slot[1])
        eng.ops.append((waits, lambda h, o=out, i=in_: h.dma_start(out=o, in_=i), (slot[0], 16)))
        self._commit(tok, reads, writes)
        self.n_ops += 1
        return tok

    def finish(self):
        waits = []
        for e in self.engs:
            for (sem, val) in e.ring:
                if val > 0:
                    waits.append((sem, val))
        self.sp.ops.append((waits, None, None))

    def emit(self):
        nc = self.nc
        with nc.Block() as block:
            def run(eng):
                def body(h):
                    for (waits, fn, inc) in eng.ops:
                        for (sem, val) in waits:
                            h.wait_ge(sem, val)
                        if fn is None:
                            continue
                        ins = fn(h)
                        if inc is not None:
                            ins.then_inc(inc[0], inc[1])
                return body
            block.tensor(run(self.pe))
            block.scalar(run(self.act))
            block.vector(run(self.dve))
            block.gpsimd(run(self.pool))
            block.sync(run(self.sp))


NLAYERS = L
DBG = 99
DEBUG_OUT = False
_DBG_RES = {}
STOP = None


def build_program():
    nc = bass.Bass("TRN2", target_bir_lowering=False)
    fw = FW(nc)
    pe, act, dve, pool, sp = fw.pe, fw.act, fw.dve, fw.pool, fw.sp

    def din(name, shape, dt=F32):
        return nc.dram_tensor(name, shape, dt, kind="ExternalInput").ap()

    def dscr(name, shape, dt=F32):
        return nc.dram_tensor(name, shape, dt, kind=("ExternalOutput" if DEBUG_OUT else "Internal")).ap()

    xin = din("xin", [D, T])
    ctxin = din("ctxin", [D, TC])
    cc = din("cc", [128, 16])
    vecs_d = din("vecs", [L, 128, NV])
    ident_d = din("ident", [128, 128])
    w_ada = din("w_ada", [L, D, INW])
    w_in = din("w_in", [L, D, INW])
    w_pc = din("w_proj_conv", [L, D, D])
    w_rg = din("w_rgate", [L, 2, 16, 64, 64])
    w_ig = din("w_igate", [L, 2, 16, 64, 64])
    w_pl = din("w_proj_lru", [L, D, D])
    w_o = din("w_out", [L, D, D])
    w_up = din("w_up", [L, D, 2 * FFN])
    w_dn = din("w_down", [L, FFN, D])
    out = nc.dram_tensor("out", [D, T], F32, kind="ExternalOutput").ap()

    class Stream:
        pass

    lat = Stream()
    ctx = Stream()
    lat.nm, lat.T, lat.col, lat.grid = "l", T, 0, True
    ctx.nm, ctx.T, ctx.col, ctx.grid = "c", TC, 1, False
    lat.tiles = [(i * NT, NT) for i in range(T // NT)]
    ctx.tiles = [(0, TC)]
    lat.xres = out
    ctx.xres = dscr("ctxres", [D, TC])
    for s in (lat, ctx):
        s.P = dscr("P" + s.nm, [D, s.T], BF16)
        s.XR = dscr("XR" + s.nm, [D, s.T])
        s.GG = dscr("GG" + s.nm, [D, s.T])
        s.SG = dscr("SG" + s.nm, [2 * D, s.T])
        s.REC = dscr("REC" + s.nm, [D, s.T])
        s.MA = dscr("MA" + s.nm, [D, s.T])
        s.Z = dscr("Z" + s.nm, [2 * FFN, s.T], BF16)

    SB_LIMIT = 229344 - 256
    apos = {"p": 16512, "base": 16512}
    acache = {}

    def sb(name, shape, dt):
        if name in acache:
            return acache[name]
        nb = (2 if dt == BF16 else 4)
        for d_ in shape[1:]:
            nb *= d_
        nb = (nb + 63) // 64 * 64
        off = apos["p"]
        assert off + nb <= SB_LIMIT, (name, off, nb)
        t_ = nc.alloc_sbuf_tensor_at(name, shape, dt, offset=off)
        apos["p"] = off + nb
        acache[name] = t_
        return t_

    class Ring:
        def __init__(self, name, shape, dtype, n):
            self.name = name
            self.t = [sb(f"{name}{i}", shape, dtype) for i in range(n)]
            self.i = 0

        def next(self):
            j = self.i % len(self.t)
            self.i += 1
            return self.t[j], (self.name, j)

    vecs = sb("vecs", [128, L, NV], F32)
    ident = sb("ident", [128, 128], F32)
    identb = sb("identb", [128, 128], BF16)
    ones = sb("ones", [128, 128], F32)
    ccs = sb("ccs", [128, 16], F32)
    modv = sb("modv", [128, L, 48, 2], F32)
    drv = sb("drv", [128, L, 6, 8, 2], F32)
    kkv = sb("kkv", [128, L, 2, 8], F32)
    kk2 = sb("kk2", [128, L, 2, 8], F32)
    carry = sb("carry", [128, 2, 8], F32)
    WG = sb("WG", [128, 4096], BF16)
    hbv = sb("hbv", [128, L, 4, 8], F32)
    kkh = sb("kkh", [128, L, 2, 8], F32)
    WA = sb("WA", [128, KC, INW], BF16)
    ARENA0 = apos["p"]
    WA_OFF = ARENA0 - 2 * KC * INW

    class Bufs:
        pass

    STG_N = 8

    def arena(tag, n, base, spec):
        apos["p"] = base
        b = Bufs()
        for (nm, kind, shape, dt, cnt) in spec:
            full = f"{tag}_{nm}"
            if kind == "ring":
                if full not in acache:
                    acache[full] = Ring(full, shape, dt, cnt)
                setattr(b, nm, acache[full])
            else:
                setattr(b, nm, sb(full, shape, dt))
        return b

    banks = [nc.alloc_psum_tensor(f"bank{i}", [128, NT], F32) for i in range(8)]
    rot = {"i": 0}

    def next_bank():
        j = rot["i"] % 6
        rot["i"] += 1
        return banks[j], ("bank", j)

    hrot = {"i": 0}

    def next_half():
        return next_bank()

    ST1, ST1K = banks[6], ("bank", 6)
    ST2, ST2K = banks[7], ("bank", 7)

    def A(o, i, f, r, w, scale=1.0, bias=0.0):
        fw.op(act, lambda h: h.activation(o, i, f, bias=bias, scale=scale), reads=r, writes=w)

    def TT(e, o, a, b_, op, r, w):
        fw.op(e, lambda h: h.tensor_tensor(o, a, b_, op), reads=r, writes=w)

    def STT(o, a, s_, b_, op0, op1, r, w):
        fw.op(dve, lambda h: h.scalar_tensor_tensor(o, a, s_, b_, op0, op1), reads=r, writes=w)

    def TS(e, o, a, s1, s2, op0, op1, r, w):
        fw.op(e, lambda h: h.tensor_scalar(o, a, s1, s2, op0, op1), reads=r, writes=w)

    def CP(e, o, i, r, w):
        fw.op(e, lambda h: h.tensor_copy(o, i), reads=r, writes=w)

    def RCP(o, i, r, w):
        fw.op(dve, lambda h: h.reciprocal(o, i), reads=r, writes=w)

    def MSET(o, v, w):
        fw.op(pool, lambda h: h.memset(o, v), writes=w)

    def SCAN(o, a, u, init, r, w):
        fw.op(dve, lambda h: h.tensor_tensor_scan(o, a, u, init, ALU.mult, ALU.add), reads=r, writes=w)

    def MM(o, lhsT, rhs, start, stop, r, w, sig=None):
        fw.op(pe, lambda h: h.matmul(o, lhsT, rhs, start=start, stop=stop), reads=r, writes=w,
              signal=(stop if sig is None else sig))

    def LD(o, i, r, w):
        fw.dma(sp, o, i, reads=r, writes=w)

    def STO(o, i, r, w):
        fw.dma(pool, o, i, reads=r, writes=w)

    def rows(ap2d, k):
        return ap2d[k * 128:(k + 1) * 128, :]

    NROWC = {"X": KC, "P": KC, "XR": KC, "GG": KC, "SG": 2 * KC, "REC": KC, "MA": KC, "Z": 2 * FC}

    def dk(name, lo, hi, rc=None):
        if rc is None:
            rc = range(NROWC[name[:-1]])
        elif isinstance(rc, int):
            rc = (rc,)
        return [(name, j, r) for j in range(lo // 256, (hi + 255) // 256) for r in rc]

    def barrier():
        toks = []
        for e in fw.engs:
            if e.count > 0:
                toks.append((e.sem, e.count))
            for (sem, val) in e.ring:
                if val > 0:
                    toks.append((sem, val))
        for e in fw.engs:
            waits = []
            for (sem, val) in toks:
                if sem is e.sem:
                    continue
                if e.known.get(id(sem), 0) < val:
                    e.known[id(sem)] = val
                    waits.append((sem, val))
            if waits:
                e.ops.append((waits, None, None))

    stg = arena("wstg", 0, SB_LIMIT - STG_N * 4096 - 64, [("r", "ring", [128, 1024], F32, STG_N)]).r
    cast_rr = {"i": 0}

    def load_weight(dst_ap_fn, src, nrows_chunks, ncols, wkey):
        for kc in range(nrows_chunks):
            for c0 in range(0, ncols, 1024):
                c1 = min(ncols, c0 + 1024)
                s_, sk = stg.next()
                LD(s_[:, :c1 - c0], src[kc * 128:(kc + 1) * 128, c0:c1], [], [sk])
                dst = dst_ap_fn(kc, c0, c1)
                j = cast_rr["i"] % 3
                cast_rr["i"] += 1
                if j != 1:
                    CP(dve, dst, s_[:, :c1 - c0], [sk], [wkey])
                else:
                    A(dst, s_[:, :c1 - c0], AF.Copy, [sk], [wkey])

    WAf = WA[:].rearrange("p k n -> p (k n)")

    LD(vecs[:], vecs_d.rearrange("l p n -> p l n"), [], ["vecs"])
    LD(ident[:], ident_d, [], ["ident"])
    LD(ccs[:], cc, [], ["ccs"])
    MSET(ones[:], 1.0, ["ones"])
    CP(dve, identb[:], ident[:], ["ident"], ["identb"])
    A(ccs[:], ccs[:], AF.Silu, ["ccs"], ["ccs"])

    def vcol(l, nm, k, w=1):
        o = _off[nm] + k
        return vecs[:, l, o:o + w]

    ccv = ccs[:].rearrange("p (k c) -> p k c", c=2)
    sbuf0 = arena("su", NT, ARENA0, [("wt", "ring", [128, KC, NT], F32, 2)])
    for l in range(L):
        for g in range(INW // NT):
            wt, wk = sbuf0.wt.next()
            LD(wt[:], w_ada[l][:, g * NT:(g + 1) * NT].rearrange("(kc p) n -> p kc n", p=128), [], [wk])
            for j in range(4):
                oc = g * 4 + j
                for kc in range(KC):
                    MM(ST1[:, oc * 2:oc * 2 + 2], wt[:, kc, j * 128:(j + 1) * 128], ccv[:, kc, :],
                       kc == 0, kc == KC - 1, [wk, "ccs"], [ST1K])
        bo = _off["b_ada"]
        for c in range(2):
            TT(dve, modv[:, l, :, c], ST1[:, 0:96].rearrange("p (o c) -> p o c", c=2)[:, :, c],
               vecs[:, l, bo:bo + 48], ALU.add, [ST1K, "vecs"], ["modv"])
        for c in range(2):
            for (di, gname, sc_o, sh_o) in [(0, "g_pre_mix", 8, 0), (3, "g_pre_ffn", 32, 24)]:
                go = _off[gname]
                TT(dve, drv[:, l, di, :, c], modv[:, l, sc_o:sc_o + 8, c], vecs[:, l, go:go + 8], ALU.mult,
                   ["modv", "vecs"], ["drv"])
                TT(dve, drv[:, l, di, :, c], drv[:, l, di, :, c], vecs[:, l, go:go + 8], ALU.add,
                   ["drv", "vecs"], ["drv"])
                CP(dve, drv[:, l, di + 1, :, c], modv[:, l, sh_o:sh_o + 8, c], ["modv"], ["drv"])
            for (di, gname, gt_o) in [(2, "g_post_mix", 16), (5, "g_post_ffn", 40)]:
                go = _off[gname]
                TT(dve, drv[:, l, di, :, c], modv[:, l, gt_o:gt_o + 8, c], vecs[:, l, go:go + 8], ALU.mult,
                   ["modv", "vecs"], ["drv"])
        for d in range(2):
            lo = _off["lam%d" % d]
            A(kkv[:, l, d, :], vecs[:, l, lo:lo + 8], AF.Exp, ["vecs"], ["kkv"], scale=-1.0)
            A(kkv[:, l, d, :], kkv[:, l, d, :], AF.Ln, ["kkv"], ["kkv"], bias=1.0)
            TS(dve, kkh[:, l, d, :], kkv[:, l, d, :], -4.0, None, ALU.mult, ALU.bypass, ["kkv"], ["kkh"])
            TS(dve, kkv[:, l, d, :], kkv[:, l, d, :], -8.0, None, ALU.mult, ALU.bypass, ["kkv", "kkh"], ["kkv"])
            for ri, nm_ in enumerate(("br", "bi")):
                bo_ = _off["%s%d" % (nm_, d)]
                TS(dve, hbv[:, l, d * 2 + ri, :], vecs[:, l, bo_:bo_ + 8], 0.5, None, ALU.mult, ALU.bypass,
                   ["vecs"], ["hbv"])
    barrier()
    stopped = {"v": STOP == "setup"}

    def DR(l, di, k, s):
        return drv[:, l, di, k, s.col:s.col + 1]

    def tiles_of(s, n):
        if s.T <= n:
            return [(0, s.T)]
        return [(i * n, n) for i in range(s.T // n)]

    def norm_pro1(b, l, s, t0, n, di, xsrc):
        x, xk = b.xt.next()
        LD(x[:, :, :n], xsrc[:, t0:t0 + n].rearrange("(kc p) n -> p kc n", p=128), dk("X" + s.nm, t0, t0 + n), [xk])
        for k in range(KC):
            q, qk = b.sq.next()
            A(q[:, :n], x[:, k, :n], AF.Square, [xk], [qk])
            MM(ST1[:, :n], ones[:], q[:, :n], k == 0, k == KC - 1, [qk, "ones"], [ST1K], sig=True)
        A(b.sd[:, :n], ST1[:, :n], AF.Sqrt, [ST1K], ["sd"], scale=1.0 / D, bias=EPS)
        RCP(b.rstd[:, :n], b.sd[:, :n], ["sd"], ["rstd"])
        return x, xk

    def norm_pro2(b, l, s, t0, n, di, x, xk):
        h, hk = b.hb.next()
        for k in range(KC):
            t, tk = b.tmp.next()
            TT(dve, t[:, :n], x[:, k, :n], b.rstd[:, :n], ALU.mult, [xk, "rstd"], [tk])
            A(h[:, k, :n], t[:, :n], AF.Identity, [tk, "drv"], [(hk, k)],
              scale=DR(l, di, k, s), bias=DR(l, di + 1, k, s))
        return h, hk

    pend = {}

    def post_oc(b, oc, bk, bkk, n):
        A(b.yb[:, oc, :n], bk[:, :n], AF.Copy, [bkk], [("yb", oc)])
        q, qk = b.sq.next()
        A(q[:, :n], bk[:, :n], AF.Square, [bkk], [qk])
        if oc > 0:
            pq, pqk = pend["q"]
            MM(ST1[:, :n], ones[:], pq[:, :n], oc == 1, False, [pqk, "ones"], [ST1K], sig=True)
        pend["q"] = (q, qk)
        if oc == KC - 1:
            MM(ST1[:, :n], ones[:], q[:, :n], False, True, [qk, "ones"], [ST1K], sig=True)

    def post_finish(b, l, s, t0, n, di, xsrc, xdst):
        A(b.sd[:, :n], ST1[:, :n], AF.Sqrt, [ST1K], ["sd"], scale=1.0 / D, bias=EPS)
        RCP(b.rstd[:, :n], b.sd[:, :n], ["sd"], ["rstd"])
        for k in range(KC):
            xl, xlk = b.ld32.next()
            LD(xl[:, :n], rows(xsrc, k)[:, t0:t0 + n], dk("X" + s.nm, t0, t0 + n, k), [xlk])
            t, tk = b.tmp.next()
            STT(t[:, :n], b.yb[:, k, :n], DR(l, di, k, s), b.rstd[:, :n], ALU.mult, ALU.mult,
                [("yb", k), "rstd", "drv"], [tk])
            o, ok = b.st32.next()
            TT(pool, o[:, :n], t[:, :n], xl[:, :n], ALU.add, [tk, xlk], [ok])
            STO(rows(xdst, k)[:, t0:t0 + n], o[:, :n], [ok], dk("X" + s.nm, t0, t0 + n, k))

    def phase1(l, s, xsrc, full):
        N = NT
        b = arena("p1", N, ARENA0, [
            ("xt", "ring", [128, KC, N], F32, 1), ("sq", "ring", [128, N], F32, 3), ("tmp", "ring", [128, N], F32, 3),
            ("hb", "ring", [128, KC, N], BF16, 2), ("st32", "ring", [128, N], F32, 3), ("st16", "ring", [128, N], BF16, 2),
            ("sgr", "ring", [128, N], F32, 2), ("sd", "t", [128, N], F32, 1), ("rstd", "t", [128, N], F32, 1)])
        if full:
            order = []
            for j in range(8):
                order += [8 + j, j]
            order += list(range(16, 24)) + list(range(32, 48)) + list(range(24, 32))
        else:
            order = list(range(16, 24))
        tl = tiles_of(s, N)
        st = norm_pro1(b, l, s, tl[0][0], tl[0][1], 0, xsrc)
        nxt_h = norm_pro2(b, l, s, tl[0][0], tl[0][1], 0, *st)
        for ti, (t0, n) in enumerate(tl):
            h, hk = nxt_h
            nx = tl[ti + 1] if ti + 1 < len(tl) else None
            sgk_of = {}
            for oi, oc in enumerate(order):
                if nx and oi == len(order) // 4:
                    st = norm_pro1(b, l, s, nx[0], nx[1], 0, xsrc)
                if nx and oi == (2 * len(order)) // 3:
                    nxt_h = norm_pro2(b, l, s, nx[0], nx[1], 0, *st)
                bk, bkk = next_bank()
                for kc in range(KC):
                    MM(bk[:, :n], WA[:, kc, oc * 128:(oc + 1) * 128], h[:, kc, :n], kc == 0, kc == KC - 1,
                       [(hk, kc), "WA"], [bkk])
                if 8 <= oc < 16:
                    g, gk = b.sgr.next()
                    A(g[:, :n], bk[:, :n], AF.Sigmoid, [bkk], [gk])
                    sgk_of[oc - 8] = (g, gk)
                elif oc < 8:
                    g, gk = sgk_of[oc]
                    o, ok = b.st16.next()
                    TT(dve, o[:, :n], bk[:, :n], g[:, :n], ALU.mult, [bkk, gk], [ok])
                    STO(rows(s.P, oc)[:, t0:t0 + n], o[:, :n], [ok], dk("P" + s.nm, t0, t0 + n, oc))
                elif oc < 24:
                    o, ok = b.st32.next()
                    A(o[:, :n], bk[:, :n], AF.Copy, [bkk], [ok])
                    STO(rows(s.XR, oc - 16)[:, t0:t0 + n], o[:, :n], [ok], dk("XR" + s.nm, t0, t0 + n, oc - 16))
                elif oc < 32:
                    o, ok = b.st32.next()
                    A(o[:, :n], bk[:, :n], AF.Gelu, [bkk], [ok])
                    STO(rows(s.GG, oc - 24)[:, t0:t0 + n], o[:, :n], [ok], dk("GG" + s.nm, t0, t0 + n, oc - 24))
                else:
                    o, ok = b.st32.next()
                    A(o[:, :n], bk[:, :n], AF.Sigmoid, [bkk], [ok])
                    STO(rows(s.SG, oc - 32)[:, t0:t0 + n], o[:, :n], [ok], dk("SG" + s.nm, t0, t0 + n, oc - 32))

    def gate_w(d, ri, k):
        o = ((d * 2 + ri) * 8 + k) * 128
        return WG[:, o:o + 128]

    def lru_spec(N):
        return [("xw", "t", [128, KC, N + 4], F32, 1), ("xc", "t", [128, KC, N], F32, 1),
                ("xcb", "t", [128, KC, N], BF16, 1), ("ra", "t", [128, KC, N], F32, 1),
                ("iu", "t", [128, KC, N], F32, 1), ("a2", "t", [128, KC, N], F32, 1),
                ("rec", "t", [128, KC, N], F32, 1)]

    def lru_tile(b, l, s, t0, n, d, part="AB"):
        allk = lambda nm: [(nm, k) for k in range(KC)]
        if "A" in part:
            lru_part_a(b, l, s, t0, n, d, allk)
        if "B" in part:
            lru_part_b(b, l, s, t0, n, d, allk)

    def lru_part_a(b, l, s, t0, n, d, allk):
        lo, hi = t0 - 1, t0 + n + 2
        clo, chi = max(lo, 0), min(hi, s.T)
        if clo > lo or chi < hi:
            MSET(b.xw[:, :, :n + 3], 0.0, ["xw"])
        LD(b.xw[:, :, clo - lo:chi - lo], s.XR[:, clo:chi].rearrange("(kc p) n -> p kc n", p=128),
           dk("XR" + s.nm, clo, chi), ["xw"])
        for k in range(KC):
            lw = _off["lru_w"] + k * 4
            TS(dve, b.xc[:, k, :n], b.xw[:, k, 0:n], vecs[:, l, lw:lw + 1], vcol(l, "lru_b", k), ALU.mult, ALU.add,
               ["xw", "vecs"], [("xc", k)])
            for j in range(1, 4):
                STT(b.xc[:, k, :n], b.xw[:, k, j:j + n], vecs[:, l, lw + j:lw + j + 1], b.xc[:, k, :n],
                    ALU.mult, ALU.add, ["xw", ("xc", k), "vecs"], [("xc", k)])
        if DBG <= 7:
            return
        CP(dve, b.xcb[:, :, :n], b.xc[:, :, :n], allk("xc"), ["xcb"])
        if DBG <= 8:
            return
        for k in range(KC):
            for ri, dst in ((0, b.ra), (1, b.iu)):
                hb_, hk_ = next_half()
                MM(hb_[:, :n], gate_w(d, ri, k), b.xcb[:, k, :n], True, True, ["xcb", "WG"], [hk_])
                A(dst[:, k, :n], hb_[:, :n], AF.Tanh, [hk_, "hbv"], [("ra" if ri == 0 else "iu", k)],
                  scale=0.5, bias=hbv[:, l, d * 2 + ri, k:k + 1])
        if DBG <= 9:
            return
        for k in range(KC):
            TS(dve, b.ra[:, k, :n], b.ra[:, k, :n], kkh[:, l, d, k:k + 1], kkh[:, l, d, k:k + 1], ALU.mult, ALU.add,
               [("ra", k), "kkh"], [("ra", k)])
        A(b.a2[:, :, :n], b.ra[:, :, :n], AF.Exp, allk("ra"), allk("a2"), scale=2.0)
        A(b.ra[:, :, :n], b.ra[:, :, :n], AF.Exp, allk("ra"), allk("ra"))
        if DBG <= 9.3:
            return
        A(b.a2[:, :, :n], b.a2[:, :, :n], AF.Sqrt, allk("a2"), allk("a2"), scale=-0.25, bias=0.25)

    def lru_part_b(b, l, s, t0, n, d, allk):
        STT(b.iu[:, :, :n], b.iu[:, :, :n], 1.0, b.xc[:, :, :n], ALU.add, ALU.mult, allk("iu") + allk("xc"), allk("iu"))
        TT(dve, b.iu[:, :, :n], b.iu[:, :, :n], b.a2[:, :, :n], ALU.mult, allk("iu") + allk("a2"), allk("iu"))
        if DBG <= 9.6:
            return
        for k in range(KC):
            if d == 0:
                SCAN(b.rec[:, k, :n], b.ra[:, k, :n], b.iu[:, k, :n], carry[:, d, k:k + 1],
                     [("ra", k), ("iu", k), ("carry", d)], [("rec", k)])
            else:
                SCAN(b.rec[:, k, n - 1::-1], b.ra[:, k, n - 1::-1], b.iu[:, k, n - 1::-1], carry[:, d, k:k + 1],
                     [("ra", k), ("iu", k), ("carry", d)], [("rec", k)])
        edge = n - 1 if d == 0 else 0
        CP(pool, carry[:, d, :], b.rec[:, :, edge], allk("rec"), [("carry", d)])

    N2A = 256
    WB_PC = 0
    WB_DG = 8192
    WB_SZ = 8192 + 31744

    def arena2a():
        return arena("p2a", N2A, WA_OFF, [("WB", "t", [128, WB_SZ], BF16, 1)] + lru_spec(N2A) + [
            ("pw", "ring", [128, KC, N2A + 32], BF16, 2), ("yb", "t", [128, KC, N2A], F32, 1),
            ("sq", "ring", [128, N2A], F32, 3), ("tmp", "ring", [128, N2A], F32, 3),
            ("hb", "t", [128, KC, N2A], BF16, 1), ("ld32", "ring", [128, N2A], F32, 3),
            ("st32", "ring", [128, N2A], F32, 3), ("sd", "t", [128, N2A], F32, 1), ("rstd", "t", [128, N2A], F32, 1),
            ("mean", "t", [128, N2A], F32, 1), ("nmr", "t", [128, N2A], F32, 1)])

    def phase2a(l, s, full):
        b = arena2a()
        WB = b.WB
        allk = lambda nm: [(nm, k) for k in range(KC)]

        def lru(t0, n, part="AB"):
            lru_tile(b, l, s, t0, n, 0, part)
            if DBG <= 10 or "B" not in part:
                return
            STO(s.REC[:, t0:t0 + n].rearrange("(kc p) n -> p kc n", p=128), b.rec[:, :, :n], allk("rec"),
                dk("REC" + s.nm, t0, t0 + n))

        def conformer(t0, n, part):
            if part == 1:
                conf1(t0, n)
            else:
                conf2(t0, n)

        pwst = {}

        def conf0(t0, n):
            pwt, pwk = b.pw.next()
            lo, hi = t0 - 15, t0 + n + 15
            clo, chi = max(lo, 0), min(hi, s.T)
            if clo > lo or chi < hi:
                MSET(pwt[:, :, :n + 30], 0.0, [pwk])
            LD(pwt[:, :, clo - lo:chi - lo], s.P[:, clo:chi].rearrange("(kc p) n -> p kc n", p=128),
               dk("P" + s.nm, clo, chi), [pwk])
            pwst[t0] = (pwt, pwk)

        def conf1(t0, n):
            pwt, pwk = pwst.pop(t0)
            if DBG <= 12.1:
                return
            for k in range(KC):
                bk, bkk = next_half()
                for j in range(CONV_K):
                    o = WB_DG + (k * CONV_K + j) * 128
                    MM(bk[:, :n], WB[:, o:o + 128], pwt[:, k, j:j + n], j == 0, j == CONV_K - 1, [pwk, ("WBd", k, j)], [bkk])
                if DBG <= 12.2:
                    continue
                A(b.yb[:, k, :n], bk[:, :n], AF.Copy, [bkk], [("yb", k)])
                q, qk = b.sq.next()
                A(q[:, :n], bk[:, :n], AF.Square, [bkk], [qk])
                if k > 0:
                    pq, pqk = pend["cq"]
                    MM(ST1[:, :n], ones[:], b.yb[:, k - 1, :n], k == 1, False, [("yb", k - 1), "ones"], [ST1K], sig=True)
                    MM(ST2[:, :n], ones[:], pq[:, :n], k == 1, False, [pqk, "ones"], [ST2K], sig=True)
                pend["cq"] = (q, qk)
                if k == KC - 1:
                    MM(ST1[:, :n], ones[:], b.yb[:, k, :n], False, True, [("yb", k), "ones"], [ST1K], sig=True)
                    MM(ST2[:, :n], ones[:], q[:, :n], False, True, [qk, "ones"], [ST2K], sig=True)

        def conf2(t0, n):
            A(b.mean[:, :n], ST1[:, :n], AF.Copy, [ST1K], ["mean"], scale=1.0 / D)
            A(b.nmr[:, :n], ST1[:, :n], AF.Square, [ST1K], ["nmr"], scale=1.0 / D)
            STT(b.sd[:, :n], ST2[:, :n], 1.0 / D, b.nmr[:, :n], ALU.mult, ALU.subtract, [ST2K, "nmr"], ["sd"])
            A(b.sd[:, :n], b.sd[:, :n], AF.Sqrt, ["sd"], ["sd"], bias=EPS)
            RCP(b.rstd[:, :n], b.sd[:, :n], ["sd"], ["rstd"])
            STT(b.nmr[:, :n], b.mean[:, :n], -1.0, b.rstd[:, :n], ALU.mult, ALU.mult, ["mean", "rstd", "nmr"], ["nmr"])
            for k in range(KC):
                t, tk = b.tmp.next()
                TT(dve, t[:, :n], b.yb[:, k, :n], b.rstd[:, :n], ALU.mult, [("yb", k), "rstd"], [tk])
                TT(pool, t[:, :n], t[:, :n], b.nmr[:, :n], ALU.add, [tk, "nmr"], [tk])
                A(b.hb[:, k, :n], t[:, :n], AF.Silu, [tk, "vecs"], [("hb", k)],
                  scale=vcol(l, "ln_g", k), bias=vcol(l, "ln_b", k))
            if DBG <= 12.4:
                return
            for oc in range(KC):
                sg_, sgk_ = b.ld32.next()
                LD(sg_[:, :n], rows(s.SG, oc)[:, t0:t0 + n], dk("SG" + s.nm, t0, t0 + n, oc), [sgk_])
                bk, bkk = next_half()
                for kc in range(KC):
                    o = WB_PC + kc * 1024 + oc * 128
                    MM(bk[:, :n], WB[:, o:o + 128], b.hb[:, kc, :n], kc == 0, kc == KC - 1, [("hb", kc), "WBp"], [bkk])
                if DBG <= 12.5:
                    continue
                o_, ok_ = b.st32.next()
                TT(dve, o_[:, :n], bk[:, :n], sg_[:, :n], ALU.mult, [bkk, sgk_], [ok_])
                if DBG <= 12.6:
                    continue
                STO(rows(s.MA, oc)[:, t0:t0 + n], o_[:, :n], [ok_], dk("MA" + s.nm, t0, t0 + n, oc))

        tl = tiles_of(s, N2A)
        if full:
            conf0(*tl[0])
        lru(*tl[0])
        for i, (t0, n) in enumerate(tl):
            if full:
                conformer(t0, n, 1)
            if i + 1 < len(tl):
                lru(*tl[i + 1], part="A")
                if full:
                    conf0(*tl[i + 1])
            if full:
                conformer(t0, n, 2)
            if i + 1 < len(tl):
                lru(*tl[i + 1], part="B")

    N2B = 256

    def phase2b(l, s, full, xsrc, xdst):
        N = N2B
        b = arena("p2b", N, WA_OFF + 2 * 2 * KC * D, lru_spec(N) + [
            ("rf", "t", [128, KC, N], F32, 1), ("gg", "t", [128, KC, N], F32, 1),
            ("qb", "ring", [128, KC, N], BF16, 2), ("hb", "t", [128, KC, N], BF16, 1),
            ("ld32", "ring", [128, N], F32, 6), ("tmp", "ring", [128, N], F32, 3), ("yb", "t", [128, KC, N], F32, 1),
            ("sq", "ring", [128, N], F32, 3), ("st32", "ring", [128, N], F32, 3),
            ("sd", "t", [128, N], F32, 1), ("rstd", "t", [128, N], F32, 1)])
        allk = lambda nm: [(nm, k) for k in range(KC)]

        def front(t0, n, part="AB"):
            if "A" in part and full:
                LD(b.rf[:, :, :n], s.REC[:, t0:t0 + n].rearrange("(kc p) n -> p kc n", p=128),
                   dk("REC" + s.nm, t0, t0 + n), ["rf"])
                LD(b.gg[:, :, :n], s.GG[:, t0:t0 + n].rearrange("(kc p) n -> p kc n", p=128),
                   dk("GG" + s.nm, t0, t0 + n), ["gg"])
            lru_tile(b, l, s, t0, n, 1, part)
            if "B" not in part:
                return None
            if not full:
                return None
            TT(dve, b.rec[:, :, :n], b.rec[:, :, :n], b.rf[:, :, :n], ALU.add, allk("rec") + ["rf"], allk("rec"))
            q_, qk_ = b.qb.next()
            TT(dve, q_[:, :, :n], b.rec[:, :, :n], b.gg[:, :, :n], ALU.mult, allk("rec") + ["gg"], [qk_])
            return q_, qk_

        def back(t0, n, q_, qk_):
            for oc in range(KC):
                sg_, sgk_ = b.ld32.next()
                LD(sg_[:, :n], rows(s.SG, 8 + oc)[:, t0:t0 + n], dk("SG" + s.nm, t0, t0 + n, 8 + oc), [sgk_])
                ma, mak = b.ld32.next()
                LD(ma[:, :n], rows(s.MA, oc)[:, t0:t0 + n], dk("MA" + s.nm, t0, t0 + n, oc), [mak])
                bk, bkk = next_half()
                for kc in range(KC):
                    MM(bk[:, :n], WAf[:, kc * D + oc * 128:kc * D + (oc + 1) * 128], q_[:, kc, :n], kc == 0,
                       kc == KC - 1, [qk_, "WA"], [bkk])
                t, tk = b.tmp.next()
                TT(dve, t[:, :n], bk[:, :n], sg_[:, :n], ALU.mult, [bkk, sgk_], [tk])
                TT(pool, b.hb[:, oc, :n], t[:, :n], ma[:, :n], ALU.add, [tk, mak], [("hb", oc)])
            for oc in range(KC):
                bk, bkk = next_half()
                for kc in range(KC):
                    MM(bk[:, :n], WAf[:, (KC + kc) * D + oc * 128:(KC + kc) * D + (oc + 1) * 128], b.hb[:, kc, :n],
                       kc == 0, kc == KC - 1, [("hb", kc), "WA"], [bkk])
                post_oc(b, oc, bk, bkk, n)

        tl = list(reversed(tiles_of(s, N)))
        cur = front(*tl[0])
        for i, (t0, n) in enumerate(tl):
            if full:
                back(t0, n, *cur)
            nxt = None
            if i + 1 < len(tl):
                front(*tl[i + 1], part="A")
                nxt = front(*tl[i + 1], part="B")
            if full:
                post_finish(b, l, s, t0, n, 2, xsrc, xdst)
            cur = nxt

    def phase3a(l, s, xsrc):
        N = NT
        b = arena("p3a", N, ARENA0, [
            ("xt", "ring", [128, KC, N], F32, 1), ("sq", "ring", [128, N], F32, 3), ("tmp", "ring", [128, N], F32, 3),
            ("hb", "ring", [128, KC, N], BF16, 2), ("st16", "ring", [128, N], BF16, 6),
            ("sd", "t", [128, N], F32, 1), ("rstd", "t", [128, N], F32, 1)])
        ev = 0
        tl = tiles_of(s, N)
        st = norm_pro1(b, l, s, tl[0][0], tl[0][1], 3, xsrc)
        nxt_h = norm_pro2(b, l, s, tl[0][0], tl[0][1], 3, *st)
        for ti, (t0, n) in enumerate(tl):
            h, hk = nxt_h
            nx = tl[ti + 1] if ti + 1 < len(tl) else None
            for oc in range(2 * FC):
                if nx and oc == FC // 2:
                    st = norm_pro1(b, l, s, nx[0], nx[1], 3, xsrc)
                if nx and oc == (4 * FC) // 3:
                    nxt_h = norm_pro2(b, l, s, nx[0], nx[1], 3, *st)
                bk, bkk = next_bank()
                for kc in range(KC):
                    MM(bk[:, :n], WA[:, kc, oc * 128:(oc + 1) * 128], h[:, kc, :n], kc == 0, kc == KC - 1,
                       [(hk, kc), "WA"], [bkk])
                o, ok = b.st16.next()
                if ev % 2 == 0:
                    A(o[:, :n], bk[:, :n], AF.Copy, [bkk], [ok])
                else:
                    CP(dve, o[:, :n], bk[:, :n], [bkk], [ok])
                ev += 1
                STO(rows(s.Z, oc)[:, t0:t0 + n], o[:, :n], [ok], dk("Z" + s.nm, t0, t0 + n, oc))

    def WDN(j, oc):
        o = j * D + oc * 128
        return WAf[:, o:o + 128]

    N3B = 256

    def arena3b():
        N = N3B
        return arena("p3b", N, WA_OFF + 2 * FC * D, [
            ("DG", "t", [128, 2 * FC * 9 * 128], BF16, 1),
            ("zw", "ring", [128, N + 2 * GW], BF16, 12), ("sl", "ring", [128, N], F32, 2),
            ("fb", "t", [128, FC, N], BF16, 1), ("yb", "t", [128, KC, N], F32, 1), ("sq", "ring", [128, N], F32, 3),
            ("tmp", "ring", [128, N], F32, 2), ("ld32", "ring", [128, N], F32, 2), ("st32", "ring", [128, N], F32, 2),
            ("sd", "t", [128, N], F32, 1), ("rstd", "t", [128, N], F32, 1)])

    def phase3b(l, s, xsrc, xdst):
        b = arena3b()
        DG = b.DG

        def dg(ch, tap):
            o = (ch * 9 + tap) * 128
            return DG[:, o:o + 128]

        for (t0, n) in tiles_of(s, N3B):
            for j in range(FC):
                bks = {}
                for half in (1, 0):
                    ch = half * FC + j
                    z, zk = b.zw.next()
                    bk, bkk = next_bank()
                    bks[half] = (bk, bkk)
                    if s.grid:
                        lo, hi = t0 - GW, t0 + n + GW
                        clo, chi = max(lo, 0), min(hi, s.T)
                        if clo > lo or chi < hi:
                            MSET(z[:, :n + 2 * GW], 0.0, [zk])
                        LD(z[:, clo - lo:chi - lo], rows(s.Z, ch)[:, clo:chi], dk("Z" + s.nm, clo, chi, ch), [zk])
                        nr = n // GW
                        zv = z[:, :n + 2 * GW].rearrange("p (r c) -> p r c", c=GW)
                        pv = bk[:, :n].rearrange("p (r c) -> p r c", c=GW)
                        taps = [(0, 0)] + [(dy, dx) for dy in (-1, 0, 1) for dx in (-1, 0, 1) if (dy, dx) != (0, 0)]
                        for i, (dy, dx) in enumerate(taps):
                            c0, c1 = max(0, -dx), GW - max(0, dx)
                            MM(pv[:, :, c0:c1], dg(ch, (dy + 1) * 3 + dx + 1),
                               zv[:, 1 + dy:1 + dy + nr, c0 + dx:c1 + dx], i == 0, i == len(taps) - 1,
                               [zk, ("DG", ch * 9 + (dy + 1) * 3 + dx + 1)], [bkk])
                    else:
                        MSET(z[:, :n + 2], 0.0, [zk])
                        LD(z[:, 1:1 + n], rows(s.Z, ch)[:, t0:t0 + n], dk("Z" + s.nm, t0, t0 + n, ch), [zk])
                        for i, dx in enumerate((0, -1, 1)):
                            MM(bk[:, :n], dg(ch, 3 + dx + 1), z[:, 1 + dx:1 + dx + n], i == 0, i == 2, [zk, ("DG", ch * 9 + 3 + dx + 1)], [bkk])
                (bv, bvk), (bg, bgk) = bks[0], bks[1]
                sl, slk = b.sl.next()
                A(sl[:, :n], bg[:, :n], AF.Silu, [bgk], [slk])
                TT(dve, b.fb[:, j, :n], bv[:, :n], sl[:, :n], ALU.mult, [bvk, slk], [("fb", j)])
            for oc in range(KC):
                bk, bkk = next_bank()
                for j in range(FC):
                    MM(bk[:, :n], WDN(j, oc), b.fb[:, j, :n], j == 0, j == FC - 1, [("fb", j), "WA"], [bkk])
                post_oc(b, oc, bk, bkk, n)
            post_finish(b, l, s, t0, n, 5, xsrc, xdst)

    def load_p1(l):
        load_weight(lambda kc, c0, c1: WA[:, kc, c0:c1], w_in[l], KC, INW, "WA")

    def load_p2a(l):
        WB = arena2a().WB
        load_weight(lambda kc, c0, c1: WB[:, WB_PC + kc * 1024 + c0:WB_PC + kc * 1024 + c1], w_pc[l], KC, D, "WBp")
        dwc = _off["dw_conv"]
        for k in range(KC):
            for j in range(CONV_K):
                o = WB_DG + (k * CONV_K + j) * 128
                wi = dwc + k * CONV_K + j
                if (k * CONV_K + j) % 3 != 2:
                    TS(dve, WB[:, o:o + 128], identb[:], vecs[:, l, wi:wi + 1], None, ALU.mult, ALU.bypass,
                       ["identb", "vecs"], [("WBd", k, j)])
                else:
                    A(WB[:, o:o + 128], identb[:], AF.Copy, ["identb", "vecs"], [("WBd", k, j)],
                      scale=vecs[:, l, wi:wi + 1])
        for d in range(2):
            for ri, wsrc in enumerate((w_rg, w_ig)):
                s_, sk = stg.next()
                MSET(s_[:, :1024], 0.0, [sk])
                sv = s_[:, :1024].rearrange("p (k j) -> p k j", j=128)
                src = wsrc[l, d].rearrange("(k two) i j -> two i k j", two=2)
                LD(sv[0:64, :, 0:64], src[0], [], [sk])
                LD(sv[64:128, :, 64:128], src[1], [], [sk])
                o = (d * 2 + ri) * 1024
                CP(dve, WG[:, o:o + 1024], s_[:, :1024], [sk], ["WG"])

    def load_p2b(l):
        load_weight(lambda kc, c0, c1: WAf[:, kc * D + c0:kc * D + c1], w_pl[l], KC, D, "WA")
        load_weight(lambda kc, c0, c1: WAf[:, (KC + kc) * D + c0:(KC + kc) * D + c1], w_o[l], KC, D, "WA")

    def load_p3a(l):
        load_weight(lambda kc, c0, c1: WA[:, kc, c0:c1], w_up[l], KC, 2 * FFN, "WA")

    def load_p3b(l):
        load_weight(lambda j, c0, c1: WAf[:, j * D + c0:j * D + c1], w_dn[l], FC, D, "WA")
        DG = arena3b().DG
        dwo = _off["dw_ffn"]
        for i in range(2 * FC * 9):
            if i % 3 != 2:
                TS(dve, DG[:, i * 128:(i + 1) * 128], identb[:], vecs[:, l, dwo + i:dwo + i + 1], None, ALU.mult,
                   ALU.bypass, ["identb", "vecs"], [("DG", i)])
            else:
                A(DG[:, i * 128:(i + 1) * 128], identb[:], AF.Copy, ["identb", "vecs"], [("DG", i)],
                  scale=vecs[:, l, dwo + i:dwo + i + 1])

    ckeys = [("carry", d) for d in range(2)]
    def stop_at(tag):
        if STOP == tag:
            stopped["v"] = True
        return stopped["v"]

    for l in range(NLAYERS):
        if stopped["v"]:
            break
        last = (l == L - 1)
        xs_l = xin if l == 0 else out
        xs_c = ctxin if l == 0 else ctx.xres
        load_p1(l)
        barrier()
        if stop_at("w1"):
            break
        phase1(l, ctx, xs_c, not last)
        if stop_at("p1c"):
            break
        phase1(l, lat, xs_l, True)
        barrier()
        if stop_at("p1"):
            break
        load_p2a(l)
        fw.op(pool, lambda h: h.memset(carry[:], 0.0), reads=ckeys, writes=ckeys)
        barrier()
        if stop_at("w2a"):
            break
        phase2a(l, ctx, not last)
        if stop_at("p2ac"):
            break
        phase2a(l, lat, True)
        barrier()
        if stop_at("p2a"):
            break
        load_p2b(l)
        barrier()
        phase2b(l, ctx, not last, xs_c, ctx.xres)
        if stop_at("p2bc"):
            break
        phase2b(l, lat, True, xs_l, out)
        barrier()
        if stop_at("p2b"):
            break
        load_p3a(l)
        barrier()
        if not last:
            phase3a(l, ctx, ctx.xres)
        phase3a(l, lat, out)
        barrier()
        if stop_at("p3a"):
            break
        load_p3b(l)
        barrier()
        if not last:
            phase3b(l, ctx, ctx.xres, ctx.xres)
        if stop_at("p3bc"):
            break
        phase3b(l, lat, out, out)
        barrier()

    fw.finish()
    fw.emit()
    return nc, fw


def _pc(v):
    v = np.asarray(v, np.float32)
    return np.ascontiguousarray(v.reshape(-1, 128).T)


def _pack_vecs(inp, l):
    cols = [None] * 0
    parts = []
    parts.append(_pc(inp["g_pre_mix"][l]))
    parts.append(_pc(inp["g_post_mix"][l]))
    parts.append(_pc(inp["g_pre_ffn"][l]))
    parts.append(_pc(inp["g_post_ffn"][l]))
    parts.append(_pc(inp["ln_conv_g"][l]))
    parts.append(_pc(inp["ln_conv_b"][l]))
    parts.append(_pc(inp["lru_conv_b"][l]))
    parts.append(_pc(inp["b_rgate"][l, 0]))
    parts.append(_pc(inp["b_rgate"][l, 1]))
    parts.append(_pc(inp["b_igate"][l, 0]))
    parts.append(_pc(inp["b_igate"][l, 1]))
    parts.append(_pc(inp["lru_lambda"][l, 0]))
    parts.append(_pc(inp["lru_lambda"][l, 1]))
    parts.append(_pc(inp["b_ada"][l]))
    w = np.asarray(inp["lru_conv_w"][l], np.float32)
    parts.append(np.ascontiguousarray(w.reshape(4, 8, 128).transpose(2, 1, 0)).reshape(128, 32))
    w = np.asarray(inp["dw_conv"][l], np.float32)
    parts.append(np.ascontiguousarray(w.reshape(CONV_K, 8, 128).transpose(2, 1, 0)).reshape(128, 8 * CONV_K))
    w = np.asarray(inp["dw_ffn"][l], np.float32)
    parts.append(np.ascontiguousarray(w.reshape(9, 40, 128).transpose(2, 1, 0)).reshape(128, 360))
    v = np.concatenate(parts, axis=1)
    assert v.shape == (128, NV), v.shape
    return v


_CACHE = {}


def kernel(**inputs):
    inp = {k: np.asarray(v) for k, v in inputs.items()}
    if "nc" not in _CACHE:
        _CACHE["nc"] = build_program()[0]
    nc = _CACHE["nc"]
    B = inp["x"].shape[0]
    vecs = np.stack([_pack_vecs(inp, l) for l in range(L)], axis=0)
    ident = np.eye(128, dtype=np.float32)
    cctx = _pc(inp["c_ctx"])
    shared = {
        "vecs": vecs, "ident": ident,
        "w_ada": np.ascontiguousarray(inp["w_ada"], np.float32),
        "w_in": np.ascontiguousarray(inp["w_in"], np.float32),
        "w_proj_conv": np.ascontiguousarray(inp["w_proj_conv"], np.float32),
        "w_rgate": np.ascontiguousarray(inp["w_rgate"], np.float32),
        "w_igate": np.ascontiguousarray(inp["w_igate"], np.float32),
        "w_proj_lru": np.ascontiguousarray(inp["w_proj_lru"], np.float32),
        "w_out": np.ascontiguousarray(inp["w_out"], np.float32),
        "w_up": np.ascontiguousarray(inp["w_up"], np.float32),
        "w_down": np.ascontiguousarray(inp["w_down"], np.float32),
    }
    in_maps = []
    for b in range(B):
        m = dict(shared)
        m["xin"] = np.ascontiguousarray(inp["x"][b].T, np.float32)
        m["ctxin"] = np.ascontiguousarray(inp["ctx"][b].T, np.float32)
        cb = _pc(inp["c"][b])
        m["cc"] = np.ascontiguousarray(np.stack([cb, cctx], axis=2).reshape(128, 16))
        in_maps.append(m)
    res = run_bass_kernel_spmd(nc, in_maps, core_ids=list(range(B)))
    if DEBUG_OUT:
        _DBG_RES.update(res.results[0])
    outp = np.stack([np.ascontiguousarray(r["out"].T) for r in res.results], axis=0)
    return outp.astype(np.float32)
```
